# Optimizing a Trainium2 kernel written in Bass

```python
import math, functools
import jax, jax.numpy as jnp
from jax import lax
import numpy as np

D_MODEL = 2048
BATCH = 4
SEQ = 8192
DEPTH = 2

DN_ALPHA = (2 * DEPTH) ** 0.25
DN_BETA = (8 * DEPTH) ** -0.25
LN_EPS = 1e-5
ADA_INIT = 0.1

MIX_WIDTH = D_MODEL // 2

POOL_WINDOWS = (2, 4, 8, 16)
POOL_GROUPS = len(POOL_WINDOWS)
POOL_GROUP_DIM = MIX_WIDTH // POOL_GROUPS

RWKV_HEAD_DIM = 64
RWKV_HEADS = MIX_WIDTH // RWKV_HEAD_DIM
RWKV_DECAY_LORA = 64
RWKV_AAA_LORA = 64
RWKV_MV_LORA = 32
RWKV_GATE_LORA = 160
RWKV_LN_EPS = RWKV_HEAD_DIM * 1e-5

GDN_HEAD_DIM = 128
GDN_HEADS = MIX_WIDTH // GDN_HEAD_DIM
GDN_CONV = 4
GDN_CHUNK = 64
GDN_EPS = 1e-6

D_FF = ((-(-(8 * D_MODEL) // 3) + 255) // 256) * 256

N_POOL = MIX_WIDTH
N_RWKV = 3 * MIX_WIDTH + RWKV_DECAY_LORA + RWKV_AAA_LORA + RWKV_GATE_LORA
N_GDN = 4 * MIX_WIDTH + 2 * GDN_HEADS
N_GATE = 3 * D_MODEL
N_IN = N_POOL + N_RWKV + N_GDN + N_GATE
IN_SPLITS = [N_POOL, N_POOL + N_RWKV, N_POOL + N_RWKV + N_GDN]
RWKV_SPLITS = [MIX_WIDTH, 2 * MIX_WIDTH, 3 * MIX_WIDTH,
               3 * MIX_WIDTH + RWKV_DECAY_LORA,
               3 * MIX_WIDTH + RWKV_DECAY_LORA + RWKV_AAA_LORA]

kernel_name = "hybrid_pool_rwkv7_gdn_deepnorm_adaln"


def _layer_norm(x, w, b):
    xf = x.astype(jnp.float32)
    mu = xf.mean(-1, keepdims=True)
    var = jnp.mean(jnp.square(xf - mu), -1, keepdims=True)
    return ((xf - mu) * lax.rsqrt(var + LN_EPS) * w + b).astype(x.dtype)


def _token_shift(p):
    return jnp.pad(p, ((0, 0), (1, 0), (0, 0)))[:, :-1]


def _pool_mixer(p, w_group, scale):
    B, T, _ = p.shape
    pg = p.reshape(B, T, POOL_GROUPS, POOL_GROUP_DIM)
    cs = jnp.cumsum(pg.astype(jnp.float32), axis=1)
    t = jnp.arange(T)
    pooled = []
    for g, win in enumerate(POOL_WINDOWS):
        csg = cs[:, :, g]
        prev = jnp.pad(csg, ((0, 0), (win, 0), (0, 0)))[:, :T]
        cnt = jnp.minimum(t + 1, win).astype(jnp.float32)[None, :, None]
        pooled.append((csg - prev) / cnt)
    pooled = jnp.stack(pooled, axis=2).astype(p.dtype) - pg
    y = jnp.einsum('btgc,gcd->btgd', pooled, w_group).reshape(B, T, MIX_WIDTH)
    return y * scale


def _rwkv7_scan(r, decay, k, v, kk, a):
    B, T, H, N = r.shape

    def step(S, inp):
        r_t, d_t, k_t, v_t, kk_t, a_t = inp
        sa = jnp.einsum('bhvk,bhk->bhv', S, -kk_t)
        S = (S * d_t[:, :, None, :] + sa[..., None] * (kk_t * a_t)[:, :, None, :]
             + v_t[..., None] * k_t[:, :, None, :])
        return S, jnp.einsum('bhvk,bhk->bhv', S, r_t)

    xs = tuple(jnp.moveaxis(z, 1, 0) for z in (r, decay, k, v, kk, a))
    S0 = jnp.zeros((B, H, N, N), jnp.float32)
    _, y = lax.scan(step, S0, xs)
    return jnp.moveaxis(y, 0, 1)


def _rwkv7_mixer(p, mu, w0, w2, a0, a2, g2, k_k, k_a, r_k, ln_w, ln_b, v_first, vres):
    f32 = jnp.float32
    B, T, _ = p.shape
    H, N = RWKV_HEADS, RWKV_HEAD_DIM
    xs = p + (_token_shift(p) - p) * mu
    r, k, v, w_lo, a_lo, g_lo = jnp.split(xs, RWKV_SPLITS, axis=-1)
    w = -jax.nn.softplus(-(w0 + jnp.tanh(w_lo) @ w2).astype(f32)) - 0.5
    a = jax.nn.sigmoid(a0 + a_lo @ a2)
    g = jax.nn.sigmoid(g_lo) @ g2
    v_layer = v
    if vres is not None:
        v0, v1, v2 = vres
        v = v + (v_first - v) * jax.nn.sigmoid(v0 + (v @ v1) @ v2)
    heads = lambda z: z.reshape(B, T, H, N).astype(f32)
    r, k, v, a = map(heads, (r, k, v, a))
    decay = jnp.exp(-jnp.exp(w)).reshape(B, T, H, N)
    kk = k * k_k.reshape(H, N).astype(f32)
    kk = kk / jnp.maximum(jnp.sqrt(jnp.sum(kk * kk, -1, keepdims=True)), 1e-12)
    k = k * (1.0 + (a - 1.0) * k_a.reshape(H, N).astype(f32))
    y = _rwkv7_scan(r, decay, k, v, kk, a)
    mean = y.mean(-1, keepdims=True)
    var = jnp.mean(jnp.square(y - mean), -1, keepdims=True)
    y = (y - mean) * lax.rsqrt(var + RWKV_LN_EPS) * ln_w.reshape(H, N) + ln_b.reshape(H, N)
    y = y + jnp.sum(r * k * r_k.astype(f32), -1, keepdims=True) * v
    y = y.reshape(B, T, MIX_WIDTH).astype(p.dtype) * g
    return y, v_layer


def _causal_dwconv(x, w):
    K = w.shape[0]
    xp = jnp.pad(x, ((0, 0), (K - 1, 0), (0, 0)))
    return lax.conv_general_dilated(xp, w[:, None, :], window_strides=(1,), padding='VALID',
                                    dimension_numbers=('NWC', 'WIO', 'NWC'),
                                    feature_group_count=x.shape[-1])


def _chunk_gated_delta_rule(q, k, v, g, beta):
    f32 = jnp.float32
    B, T, H, Dk = q.shape
    Dv = v.shape[-1]
    L = GDN_CHUNK
    NC = T // L

    def chunks(z):
        return jnp.moveaxis(z.reshape(B, NC, L, H, *z.shape[3:]), 3, 1)

    q, k, v, g, beta = map(chunks, (q, k, v, g, beta))
    G = jnp.cumsum(g, axis=-1)
    idx = jnp.arange(L)
    causal = idx[:, None] >= idx[None, :]
    strict = idx[:, None] > idx[None, :]
    diff = G[..., :, None] - G[..., None, :]
    gamma = jnp.where(causal, jnp.exp(jnp.where(causal, diff, 0.0)), 0.0)
    kb = k * beta[..., None]
    m = jnp.where(strict, jnp.einsum('bhnid,bhnjd->bhnij', kb, k) * gamma, 0.0)
    unit_lower = m + jnp.eye(L, dtype=f32)
    solve = functools.partial(lax.linalg.triangular_solve, left_side=True, lower=True,
                              unit_diagonal=True)
    u = solve(unit_lower, v * beta[..., None])
    w = solve(unit_lower, kb * jnp.exp(G)[..., None])
    a_qk = jnp.einsum('bhnid,bhnjd->bhnij', q, k) * gamma

    def step(S, inp):
        q_c, k_c, u_c, w_c, G_c, a_c = inp
        v_new = u_c - jnp.einsum('bhld,bhde->bhle', w_c, S)
        o = (jnp.einsum('bhld,bhde->bhle', q_c * jnp.exp(G_c)[..., None], S)
             + jnp.einsum('bhls,bhse->bhle', a_c, v_new))
        g_last = G_c[..., -1:]
        S = (S * jnp.exp(g_last)[..., None]
             + jnp.einsum('bhld,bhle->bhde', k_c * jnp.exp(g_last - G_c)[..., None], v_new))
        return S, o

    xs = tuple(jnp.moveaxis(z, 2, 0) for z in (q, k, u, w, G, a_qk))
    S0 = jnp.zeros((B, H, Dk, Dv), f32)
    _, o = lax.scan(step, S0, xs)
    return o.transpose(1, 0, 3, 2, 4).reshape(B, T, H, Dv)


def _gdn_mixer(p, conv_w, a_log, dt_bias, norm_w):
    f32 = jnp.float32
    B, T, _ = p.shape
    C, H, Dh = MIX_WIDTH, GDN_HEADS, GDN_HEAD_DIM
    qkv = jax.nn.silu(_causal_dwconv(p[..., :3 * C], conv_w))
    z, a_raw, b_raw = jnp.split(p[..., 3 * C:], [C, C + H], axis=-1)
    q, k, v = (t.reshape(B, T, H, Dh).astype(f32) for t in jnp.split(qkv, 3, axis=-1))
    q = q * lax.rsqrt(jnp.sum(q * q, -1, keepdims=True) + GDN_EPS) * (Dh ** -0.5)
    k = k * lax.rsqrt(jnp.sum(k * k, -1, keepdims=True) + GDN_EPS)
    g = -jnp.exp(a_log.astype(f32)) * jax.nn.softplus(a_raw.astype(f32) + dt_bias)
    beta = jax.nn.sigmoid(b_raw.astype(f32))
    o = _chunk_gated_delta_rule(q, k, v, g, beta)
    o = o * lax.rsqrt(jnp.mean(o * o, -1, keepdims=True) + GDN_EPS) * norm_w
    o = o * jax.nn.silu(z.reshape(B, T, H, Dh).astype(f32))
    return o.reshape(B, T, C).astype(p.dtype)


def setup_inputs(seed: int = 0) -> dict:
    key = jax.random.key(seed)
    ks = iter(jax.random.split(key, 48))
    f32 = jnp.float32
    L, D, C = DEPTH, D_MODEL, MIX_WIDTH

    def nrm(shape, scale):
        return jax.random.normal(next(ks), shape, f32) * scale

    def unif(shape, lo, hi):
        return jax.random.uniform(next(ks), shape, f32, lo, hi)

    x = nrm((BATCH, SEQ, D), 1.0)
    c = nrm((BATCH, D), 1.0)
    ada_w = nrm((L, D, 6 * D), ADA_INIT * D ** -0.5)
    ada_b = nrm((L, 6 * D), 0.01)
    w_in = nrm((L, D, N_IN), D ** -0.5)
    pool_w = nrm((L, POOL_GROUPS, POOL_GROUP_DIM, POOL_GROUP_DIM), POOL_GROUP_DIM ** -0.5)
    pool_scale = 1.0 + nrm((L, C), 0.02)
    rwkv_mu = unif((L, N_RWKV), 0.0, 1.0)
    rwkv_w0 = unif((L, C), -6.0, 0.0)
    rwkv_w2 = nrm((L, RWKV_DECAY_LORA, C), 0.1 * RWKV_DECAY_LORA ** -0.5)
    rwkv_a0 = nrm((L, C), 0.1)
    rwkv_a2 = nrm((L, RWKV_AAA_LORA, C), 0.1 * RWKV_AAA_LORA ** -0.5)
    rwkv_g2 = nrm((L, RWKV_GATE_LORA, C), RWKV_GATE_LORA ** -0.5)
    rwkv_k_k = 0.85 + nrm((L, C), 0.02)
    rwkv_k_a = 1.0 + nrm((L, C), 0.02)
    rwkv_r_k = nrm((L, RWKV_HEADS, RWKV_HEAD_DIM), 0.1)
    rwkv_ln_w = 1.0 + nrm((L, C), 0.02)
    rwkv_ln_b = nrm((L, C), 0.01)
    rwkv_v0 = nrm((L - 1, C), 0.1)
    rwkv_v1 = nrm((L - 1, C, RWKV_MV_LORA), C ** -0.5)
    rwkv_v2 = nrm((L - 1, RWKV_MV_LORA, C), 0.1 * RWKV_MV_LORA ** -0.5)
    gdn_conv_w = nrm((L, GDN_CONV, 3 * C), GDN_CONV ** -0.5)
    gdn_a_log = jnp.log(unif((L, GDN_HEADS), 1.0, 16.0))
    dt = jnp.exp(unif((L, GDN_HEADS), math.log(1e-3), math.log(1e-1)))
    gdn_dt_bias = dt + jnp.log(-jnp.expm1(-dt))
    gdn_norm_w = 1.0 + nrm((L, GDN_HEAD_DIM), 0.02)
    w_branch_a = nrm((L, C, D), C ** -0.5)
    w_branch_b = nrm((L, C, D), C ** -0.5)
    w_branch_c = nrm((L, C, D), C ** -0.5)
    w_out = nrm((L, D, D), DN_BETA * D ** -0.5)
    ln1_w = 1.0 + nrm((L, D), 0.02)
    ln1_b = nrm((L, D), 0.01)
    ffn_w_up = nrm((L, D, 2 * D_FF), D ** -0.5)
    ffn_w_down = nrm((L, D_FF, D), DN_BETA * D_FF ** -0.5)
    ln2_w = 1.0 + nrm((L, D), 0.02)
    ln2_b = nrm((L, D), 0.01)
    return {"x": x, "c": c, "ada_w": ada_w, "ada_b": ada_b, "w_in": w_in,
            "pool_w": pool_w, "pool_scale": pool_scale,
            "rwkv_mu": rwkv_mu, "rwkv_w0": rwkv_w0, "rwkv_w2": rwkv_w2, "rwkv_a0": rwkv_a0,
            "rwkv_a2": rwkv_a2, "rwkv_g2": rwkv_g2, "rwkv_k_k": rwkv_k_k, "rwkv_k_a": rwkv_k_a,
            "rwkv_r_k": rwkv_r_k, "rwkv_ln_w": rwkv_ln_w, "rwkv_ln_b": rwkv_ln_b,
            "rwkv_v0": rwkv_v0, "rwkv_v1": rwkv_v1, "rwkv_v2": rwkv_v2,
            "gdn_conv_w": gdn_conv_w, "gdn_a_log": gdn_a_log, "gdn_dt_bias": gdn_dt_bias,
            "gdn_norm_w": gdn_norm_w, "w_branch_a": w_branch_a, "w_branch_b": w_branch_b,
            "w_branch_c": w_branch_c, "w_out": w_out, "ln1_w": ln1_w, "ln1_b": ln1_b,
            "ffn_w_up": ffn_w_up, "ffn_w_down": ffn_w_down, "ln2_w": ln2_w, "ln2_b": ln2_b}


def reference(x, c, ada_w, ada_b, w_in, pool_w, pool_scale, rwkv_mu, rwkv_w0, rwkv_w2, rwkv_a0,
              rwkv_a2, rwkv_g2, rwkv_k_k, rwkv_k_a, rwkv_r_k, rwkv_ln_w, rwkv_ln_b, rwkv_v0,
              rwkv_v1, rwkv_v2, gdn_conv_w, gdn_a_log, gdn_dt_bias, gdn_norm_w, w_branch_a,
              w_branch_b, w_branch_c, w_out, ln1_w, ln1_b, ffn_w_up, ffn_w_down, ln2_w, ln2_b):
    cond = jax.nn.silu(c)
    v_first = None
    for l in range(DEPTH):
        mod = cond @ ada_w[l] + ada_b[l]
        sh_m, sc_m, gt_m, sh_f, sc_f, gt_f = jnp.split(mod[:, None, :], 6, axis=-1)

        h = x * (1.0 + sc_m) + sh_m
        proj = h @ w_in[l]
        p_pool, p_rwkv, p_gdn, p_gate = jnp.split(proj, IN_SPLITS, axis=-1)
        y_a = _pool_mixer(p_pool, pool_w[l], pool_scale[l])
        vres = None if l == 0 else (rwkv_v0[l - 1], rwkv_v1[l - 1], rwkv_v2[l - 1])
        y_b, v_l = _rwkv7_mixer(p_rwkv, rwkv_mu[l], rwkv_w0[l], rwkv_w2[l], rwkv_a0[l],
                                rwkv_a2[l], rwkv_g2[l], rwkv_k_k[l], rwkv_k_a[l], rwkv_r_k[l],
                                rwkv_ln_w[l], rwkv_ln_b[l], v_first, vres)
        if l == 0:
            v_first = v_l
        y_c = _gdn_mixer(p_gdn, gdn_conv_w[l], gdn_a_log[l], gdn_dt_bias[l], gdn_norm_w[l])
        g_a, g_b, g_c = jnp.split(jax.nn.sigmoid(p_gate), 3, axis=-1)
        merged = g_a * (y_a @ w_branch_a[l]) + g_b * (y_b @ w_branch_b[l]) + g_c * (y_c @ w_branch_c[l])
        x = _layer_norm(DN_ALPHA * x + (1.0 + gt_m) * (merged @ w_out[l]), ln1_w[l], ln1_b[l])

        h = x * (1.0 + sc_f) + sh_f
        gate, up = jnp.split(h @ ffn_w_up[l], 2, axis=-1)
        ffn = (jax.nn.silu(gate) * up) @ ffn_w_down[l]
        x = _layer_norm(DN_ALPHA * x + (1.0 + gt_f) * ffn, ln2_w[l], ln2_b[l])
    return x
```

```python
import numpy as np
from contextlib import ExitStack
import concourse.bass as bass
import concourse.mybir as mybir
from concourse.bass_utils import run_bass_kernel_spmd

F32 = mybir.dt.float32
BF16 = mybir.dt.bfloat16
AF = mybir.ActivationFunctionType
ALU = mybir.AluOpType

EPOCH = 16000
D = 2048
KT = 16
C = 1024
TT = 256
L = 64
NCH = TT // L
DFF = 5632
KTF = DFF // 128
N_IN = 14640
DN_ALPHA = 4.0 ** 0.25
LN_EPS = 1e-5
C0 = float(np.exp(-0.5))
R0 = 1024
G0 = 4384
Q0 = 8496


class Trk:
    __slots__ = ("name", "w", "r")

    def __init__(self, name):
        self.name = name
        self.w = None
        self.r = []


class V:
    __slots__ = ("ap", "tr")

    def __init__(self, ap, tr):
        self.ap = ap
        self.tr = tr

    def __getitem__(self, key):
        return V(self.ap[key], self.tr)


class Buf:
    def __init__(self, t, trs, split):
        self.t = t
        self.trs = trs
        self.split = split

    def __getitem__(self, i):
        assert self.split
        return V(self.t[:, i], [self.trs[i]])

    def all(self):
        return V(self.t[:], list(self.trs))


class Prog:
    ENGS = ("pe", "act", "dve", "pool", "sp")

    def __init__(self, nc, n_dma_sems=16):
        self.nc = nc
        self.es = ExitStack()
        self.ops = []
        self.n_dma_sems = n_dma_sems

    def sb(self, name, shape, dtype, split=False):
        t = self.es.enter_context(self.nc.sbuf_tensor("s_" + name, list(shape), dtype))
        n = shape[1] if split else 1
        return Buf(t, [Trk(f"{name}{i}") for i in range(n)], split)

    def ps(self, name, shape, dtype=F32):
        t = self.es.enter_context(self.nc.psum_tensor("p_" + name, list(shape), dtype))
        return Buf(t, [Trk(name)], False)

    def _rec(self, eng, fn, reads, writes, dma):
        oid = len(self.ops)
        deps = set()
        for v in reads:
            for t in v.tr:
                if t.w is not None:
                    deps.add(t.w)
                t.r.append(oid)
        for v in writes:
            for t in v.tr:
                if t.w is not None:
                    deps.add(t.w)
                deps.update(x for x in t.r if x != oid)
                t.w = oid
                t.r = []
        self.ops.append(dict(eng=eng, fn=fn, deps=sorted(deps), dma=dma))
        return oid

    def op(self, eng, fn, reads=(), writes=()):
        return self._rec(eng, fn, reads, writes, False)

    def dma(self, q, out, in_, reads=(), writes=()):
        rd = list(reads)
        wr = list(writes)
        oap, iap = out, in_
        if isinstance(out, V):
            wr.append(out)
            oap = out.ap
        if isinstance(in_, V):
            rd.append(in_)
            iap = in_.ap
        return self._rec(q, lambda e: e.dma_start(out=oap, in_=iap), rd, wr, True)

    def mm(self, out, lhsT, rhs, start=True, stop=True):
        return self.op("pe", lambda e: e.matmul(out.ap, lhsT.ap, rhs.ap, start=start, stop=stop),
                       [lhsT, rhs] + ([] if start else [out]), [out])

    def act(self, out, in_, func, bias=None, scale=None):
        kw = {}
        rd = [in_]
        if bias is not None:
            if isinstance(bias, V):
                rd.append(bias)
                kw["bias"] = bias.ap
            else:
                kw["bias"] = bias
        if scale is not None:
            if isinstance(scale, V):
                rd.append(scale)
                kw["scale"] = scale.ap
            else:
                kw["scale"] = scale
        return self.op("act", lambda e: e.activation(out=out.ap, in_=in_.ap, func=func, **kw), rd, [out])

    def tt(self, out, in0, in1, op, eng="dve"):
        return self.op(eng, lambda e: e.tensor_tensor(out=out.ap, in0=in0.ap, in1=in1.ap, op=op),
                       [in0, in1], [out])

    def ts(self, out, in0, s1, op0, s2=None, op1=None, eng="dve"):
        rd = [in0]
        a1 = s1
        if isinstance(s1, V):
            rd.append(s1)
            a1 = s1.ap
        a2 = s2
        if isinstance(s2, V):
            rd.append(s2)
            a2 = s2.ap
        if op1 is None:
            return self.op(eng, lambda e: e.tensor_scalar(out=out.ap, in0=in0.ap, scalar1=a1, scalar2=None,
                                                          op0=op0), rd, [out])
        return self.op(eng, lambda e: e.tensor_scalar(out=out.ap, in0=in0.ap, scalar1=a1, scalar2=a2,
                                                      op0=op0, op1=op1), rd, [out])

    def stt(self, out, in0, s, in1, op0, op1, eng="dve"):
        eng = "dve"
        rd = [in0, in1]
        a = s
        if isinstance(s, V):
            rd.append(s)
            a = s.ap
        return self.op(eng, lambda e: e.scalar_tensor_tensor(out=out.ap, in0=in0.ap, scalar=a, in1=in1.ap,
                                                             op0=op0, op1=op1), rd, [out])

    def copy(self, out, in_, eng="dve"):
        if eng == "act":
            return self.op("act", lambda e: e.copy(out=out.ap, in_=in_.ap), [in_], [out])
        return self.op(eng, lambda e: e.tensor_copy(out=out.ap, in_=in_.ap), [in_], [out])

    def memset(self, out, val, eng="dve"):
        return self.op(eng, lambda e: e.memset(out.ap, val), [], [out])

    def scan(self, out, d0, d1):
        return self.op("dve", lambda e: e.tensor_tensor_scan(out=out.ap, data0=d0.ap, data1=d1.ap, initial=0.0,
                                                             op0=ALU.mult, op1=ALU.add), [d0, d1], [out])

    def recip(self, out, in_):
        return self.op("dve", lambda e: e.reciprocal(out=out.ap, in_=in_.ap), [in_], [out])

    def emit(self):
        nc = self.nc
        ops = self.ops
        eng_ops = {e: [] for e in self.ENGS}
        for i, o in enumerate(ops):
            eng_ops[o["eng"]].append(i)
        sem_of = {}
        dma_sems = {}
        for e in self.ENGS:
            n_comp = sum(1 for i in eng_ops[e] if not ops[i]["dma"])
            for ep in range((n_comp + EPOCH - 1) // EPOCH):
                sem_of[(e, ep)] = self.es.enter_context(nc.semaphore(f"s_{e}_{ep}"))
            if any(ops[i]["dma"] for i in eng_ops[e]):
                dma_sems[e] = [[self.es.enter_context(nc.semaphore(f"d_{e}_{k}")), 0]
                               for k in range(self.n_dma_sems)]
        for e in self.ENGS:
            ci = 0
            di = 0
            for i in eng_ops[e]:
                o = ops[i]
                if o["dma"]:
                    slot = dma_sems[e][di % self.n_dma_sems]
                    di += 1
                    o["prev"] = (slot[0], slot[1])
                    slot[1] += 16
                    o["tick"] = (slot[0], slot[1])
                else:
                    o["tick"] = (sem_of[(e, ci // EPOCH)], ci % EPOCH + 1)
                    ci += 1
        final_waits = []
        for e in self.ENGS:
            if e in dma_sems:
                for s, v in dma_sems[e]:
                    if v > 0:
                        final_waits.append((s, v))
        self.n_inst = {e: len(eng_ops[e]) for e in self.ENGS}
        block = self.es.enter_context(nc.Block())

        def body(e_name):
            def f(eng):
                waited = {}

                def wait(sem, val):
                    k = id(sem)
                    if waited.get(k, 0) >= val:
                        return
                    eng.wait_ge(sem, val)
                    waited[k] = val

                for i in eng_ops[e_name]:
                    o = ops[i]
                    for d in o["deps"]:
                        od = ops[d]
                        if od["eng"] == e_name and not od["dma"] and e_name == "pe":
                            continue
                        s, v = od["tick"]
                        wait(s, v)
                    if o["dma"]:
                        s, v = o["prev"]
                        if v > 0:
                            wait(s, v)
                    ins = o["fn"](eng)
                    s, v = o["tick"]
                    ins.then_inc(s, 16 if o["dma"] else 1)
                if e_name == "sp":
                    for s, v in final_waits:
                        wait(s, v)
            return f

        block.tensor(body("pe"))
        block.scalar(body("act"))
        block.vector(body("dve"))
        block.gpsimd(body("pool"))
        block.sync(body("sp"))

    def close(self):
        self.es.close()


def make_consts():
    c128 = np.zeros((128, 452 + TT), np.float32)
    c128[:, 0:128] = np.eye(128, dtype=np.float32)
    c128[:, 128:256] = 1.0
    c128[0:64, 256:320] = 1.0
    c128[64:128, 320:384] = 1.0
    c128[0:64, 384] = 1.0
    c128[64:128, 385] = 1.0
    c128[0:64, 386] = -1.0
    c128[64:128, 387] = -1.0
    rm = np.ones((TT,), np.float32)
    rm[0::L] = 0.0
    c128[:, 388:388 + TT] = rm[None, :]
    for g, win in enumerate((2, 4, 8, 16)):
        for t in range(16):
            c128[:, 388 + TT + g * 16 + t] = 1.0 / min(t + 1, win)
    i = np.arange(L)[:, None]
    t = np.arange(L)[None, :]
    su = (i < t).astype(np.float32)
    u = (i <= t).astype(np.float32)
    sl = (i > t).astype(np.float32)
    NEG = -30000.0
    parts = [su, u, sl, np.eye(L, dtype=np.float32),
             np.where(t >= i, 0.0, NEG), np.where(t > i, 0.0, NEG), np.where(t < i, 0.0, NEG)]
    c64 = np.zeros((64, 7 * TT + 64), np.float32)
    for k, m in enumerate(parts):
        c64[:, k * TT:(k + 1) * TT] = np.tile(m.astype(np.float32), (1, NCH))
    c64[63, 7 * TT:7 * TT + 64] = 1.0
    return c128, c64


IN_TILES = ([(i * 128, 128) for i in range(8)]
            + [(R0 + i * 128, 128) for i in range(24)]
            + [(R0 + 3072, 128), (R0 + 3200, 128), (R0 + 3328, 32)]
            + [(G0 + i * 128, 128) for i in range(32)]
            + [(G0 + 4096, 16)]
            + [(Q0 + i * 128, 128) for i in range(48)])
T_POOL, T_R, T_K, T_V, T_LO, T_GQ, T_GK, T_GV, T_GZ, T_GAB, T_GATE = 0, 8, 16, 24, 32, 35, 43, 51, 59, 67, 68
UP_TILES = []
for _i in range(KTF):
    UP_TILES += [(_i * 128, 128), (DFF + _i * 128, 128)]
D_TILES = [(i * 128, 128) for i in range(16)]

O_MOD, O_LN1W, O_LN1B, O_LN2W, O_LN2B, O_PSC, O_MU = 0, 96, 112, 128, 144, 160, 168
O_W0, O_A0, O_KK, O_KA, O_KA1, O_RK, O_LNW, O_LNB, O_V0 = 195, 203, 211, 219, 227, 235, 243, 251, 259
O_CONV, O_GNW, NPAR = 267, 363, 364


def build(T, NL, dbg=None, NL_main=None):
    NTILES = T // TT
    nc = bass.Bass("TRN2", target_bir_lowering=False)
    P = Prog(nc)

    def din(name, shape):
        return nc.dram_tensor(name, list(shape), F32, kind="ExternalInput").ap()

    x_d = din("x", [T, D])
    c_d = din("c", [D])
    ada_w_d = din("ada_w", [2, D, 6 * D])
    ada_b_d = din("ada_b", [2, 6 * D])
    w_in_d = din("w_in", [2, D, N_IN])
    pool_w_d = din("pool_w", [2, 4, 256, 256])
    pool_scale_d = din("pool_scale", [2, C])
    mu_d = din("rwkv_mu", [2, 3360])
    w0_d = din("rwkv_w0", [2, C])
    w2_d = din("rwkv_w2", [2, 64, C])
    a0_d = din("rwkv_a0", [2, C])
    a2_d = din("rwkv_a2", [2, 64, C])
    g2_d = din("rwkv_g2", [2, 160, C])
    kk_d = din("rwkv_k_k", [2, C])
    ka_d = din("rwkv_k_a", [2, C])
    rk_d = din("rwkv_r_k", [2, C])
    lnw_d = din("rwkv_ln_w", [2, C])
    lnb_d = din("rwkv_ln_b", [2, C])
    v0_d = din("rwkv_v0", [1, C])
    v1_d = din("rwkv_v1", [1, C, 32])
    v2_d = din("rwkv_v2", [1, 32, C])
    conv_d = din("gdn_conv_w", [2, 4, 3 * C])
    alog_d = din("gdn_a_log", [2, 8])
    dtb_d = din("gdn_dt_bias", [2, 8])
    gnw_d = din("gdn_norm_w", [2, 128])
    wbr_d = [din("w_branch_a", [2, C, D]), din("w_branch_b", [2, C, D]), din("w_branch_c", [2, C, D])]
    wout_d = din("w_out", [2, D, D])
    ln1w_d = din("ln1_w", [2, D])
    ln1b_d = din("ln1_b", [2, D])
    up_d = din("ffn_w_up", [2, D, 2 * DFF])
    dn_d = din("ffn_w_down", [2, DFF, D])
    ln2w_d = din("ln2_w", [2, D])
    ln2b_d = din("ln2_b", [2, D])
    c128_d = din("c128", [128, 452 + TT])
    c64_d = din("c64", [64, 7 * TT + 64])
    out_d = nc.dram_tensor("out", [T, D], F32, kind="ExternalOutput").ap()
    mod_s = nc.dram_tensor("mod_s", [2, 6 * D], F32, kind="Internal").ap()
    dbg_d = nc.dram_tensor("dbg", [3, 128, 8, TT], BF16, kind="ExternalOutput").ap() if dbg is not None else None
    dbgh_d = nc.dram_tensor("dbgh", [128, KT, TT], BF16, kind="ExternalOutput").ap() if dbg is not None else None
    dbgc_d = nc.dram_tensor("dbgc", [128, NPAR], F32, kind="ExternalOutput").ap() if dbg is not None else None
    dbgx_d = nc.dram_tensor("dbgx", [12, 128, TT], F32, kind="ExternalOutput").ap() if dbg is not None else None
    mod_trk = [Trk("mod_s")]
    ada_trk = [Trk("ada_w")]

    xT = P.sb("xT", [128, KT, TT], F32, split=True)
    hT = P.sb("hT", [128, KT, TT], BF16, split=True)
    vfirst = P.sb("vfirst", [128, 8, TT], F32, split=True)
    yb = [P.sb(f"y{b}", [128, 8, TT], BF16, split=True) for b in range(3)]
    merged = P.sb("merged", [128, KT, TT], BF16, split=True)
    NWS = 3
    WSLOT = 6144
    wbuf = [P.sb(f"wbuf{i}", [128, WSLOT], BF16) for i in range(NWS)]
    cols = [P.sb(f"cols{l}", [128, NPAR], F32) for l in range(2)]
    g16 = [P.sb(f"g16_{l}", [16, 2], F32) for l in range(2)]
    c128 = P.sb("c128", [128, 452 + TT], F32)
    c64 = P.sb("c64", [64, 7 * TT + 64], F32)
    condT = P.sb("condT", [128, 16], F32)
    vstage = P.sb("vstage", [128, 128], F32)
    rwH = [P.sb(f"rwH{l}", [128, 8, 64], F32, split=True) for l in range(2)]
    gdS = [P.sb(f"gdS{l}", [128, 8, 128], F32, split=True) for l in range(2)]
    pooltail = [P.sb(f"ptail{l}", [128, 8, 15], F32) for l in range(2)]
    shtail = [P.sb(f"shtail{l}", [128, 27], F32) for l in range(2)]
    cvtail = [P.sb(f"cvtail{l}", [128, 24, 3], F32) for l in range(2)]
    SMALL = P.sb("SMALL", [128, 6144], BF16)
    small_s = nc.dram_tensor("small_s", [2, 128, 6144], BF16, kind="Internal").ap()
    small_trk = [[Trk("small_s0")], [Trk("small_s1")]]

    class _SV:
        def __init__(self, a, b_, rows=128):
            self.t = SMALL.t[0:rows, a:b_]
            self.trs = SMALL.trs

        def all(self):
            return V(self.t, self.trs)
    W2z = [_SV(0, 1024)] * 2
    A2z = [_SV(1024, 2048)] * 2
    G2a = [_SV(2048, 3072)] * 2
    G2b = [_SV(3072, 4096, 32)] * 2
    V1 = P.sb("V1", [128, 8, 32], BF16)
    V2 = P.sb("V2", [32, C], BF16)
    class _PW:
        t = SMALL.t[:, 4096:6144].rearrange("p (g k d) -> p g k d", g=4, k=2)
        trs = SMALL.trs

        def all(self):
            return V(self.t, self.trs)
    POOLW = [_PW()] * 2
    ARENA_N = 16896
    arena = P.sb("arena", [128, ARENA_N], F32)

    banks = [P.ps(f"bank{i}", [128, 512]) for i in range(8)]
    rr = [0]

    def nb():
        b = banks[rr[0] % 5]
        rr[0] += 1
        return b

    BANK_Y, BANK_H = banks[5], banks[6]

    ident = V(c128.t[:, 0:128], c128.trs)
    ones = V(c128.t[:, 128:256], c128.trs)
    blk = V(c128.t[:, 256:384], c128.trs)

    def mcol(i):
        return V(c128.t[:, 384 + i:385 + i], c128.trs)

    rmask = V(c128.t[:, 388:388 + TT], c128.trs)

    def c64v(k):
        return V(c64.t[:, k * TT:(k + 1) * TT], c64.trs)

    M_SU, M_U, M_SL, M_ID, GB_U, GB_SU, GB_SL = [c64v(k) for k in range(7)]
    E63 = V(c64.t[:, 7 * TT:7 * TT + 64], c64.trs)

    class Arena:
        def __init__(self):
            self.off = 0
            self.views = []

        def reset(self):
            self.off = 0

        def f32(self, name, parts, n):
            v = V(arena.t[0:parts, self.off:self.off + n], [Trk(name)])
            self.off += n
            assert self.off <= ARENA_N, (name, self.off)
            self.views.append(v)
            return v

        def bf16(self, name, parts, n):
            n32 = (n + 1) // 2
            ap = arena.t[0:parts, self.off:self.off + n32].bitcast(BF16)
            v = V(ap, [Trk(name)])
            self.off += n32
            assert self.off <= ARENA_N, (name, self.off)
            self.views.append(v)
            return v

    AR = Arena()
    fence_scr = P.sb("fence", [128, 1], F32)

    def fence(old_views, new_views):
        P.op("dve", lambda e: e.memset(fence_scr.t[:], 0.0), [], list(old_views) + list(new_views) + [fence_scr.all()])

    cyc = [0]

    def eng2():
        cyc[0] += 1
        return "dve" if cyc[0] % 2 else "pool"

    P.dma("sp", c128.all(), c128_d)
    P.dma("sp", c64.all(), c64_d)
    for l in range(2):
        P.memset(rwH[l].all(), 0.0)
        P.memset(gdS[l].all(), 0.0, eng="pool")
        P.memset(pooltail[l].all(), 0.0)
        P.memset(shtail[l].all(), 0.0, eng="pool")
        P.memset(cvtail[l].all(), 0.0)

    def load_vec_cols(src1d, n, dst, nlast=128, src_tr=None):
        if src_tr is not None:
            P.dma("sp", vstage.all()[0:n, :], V(src1d.rearrange("(j p) -> j p", p=128), src_tr))
        elif nlast != 128:
            P.memset(vstage.all(), 0.0)
            P.dma("sp", vstage.all()[0:n - 1, :], src1d[0:(n - 1) * 128].rearrange("(j p) -> j p", p=128))
            P.dma("sp", vstage.all()[n - 1:n, 0:nlast],
                  src1d[(n - 1) * 128:(n - 1) * 128 + nlast].rearrange("(j p) -> j p", p=nlast))
        else:
            P.dma("sp", vstage.all()[0:n, :], src1d.rearrange("(j p) -> j p", p=128))
        b = nb()
        P.mm(b.all()[:, 0:n], vstage.all()[0:n, :], ident[0:n, 0:n])
        P.copy(dst, b.all()[:, 0:n], eng="act")

    def colv(l, off, n=1):
        return V(cols[l].t[:, off:off + n], cols[l].trs)

    load_vec_cols(c_d, 16, condT.all())
    P.act(condT.all(), condT.all(), AF.Silu)

    AR.reset()
    stg = [AR.f32(f"stg{i}", 128, 5632) for i in range(2)]
    stb = [AR.bf16(f"stb{i}", 128, 5632) for i in range(2)]
    modrow = [P.sb(f"modrow{i}", [1, 256], F32).all() for i in range(2)]
    modtmp = P.sb("modtmp", [128, 96], F32).all()
    prologue_views = list(AR.views)
    sidx = [0]

    for l in range(NL):
        for cb in range(48):
            st = stg[sidx[0] % 2]
            sidx[0] += 1
            stv = V(st.ap[:, 0:KT * 256].rearrange("p (k c) -> p k c", k=KT), st.tr)
            P.dma("sp", stv, V(ada_w_d[l][:, cb * 256:(cb + 1) * 256].rearrange("(k p) c -> p k c", p=128), ada_trk))
            b = nb()
            for kt in range(KT):
                P.mm(b.all()[0:1, 0:256], condT.all()[:, kt:kt + 1], stv[:, kt, :], start=(kt == 0), stop=(kt == KT - 1))
            mr = modrow[cb % 2]
            P.copy(mr, b.all()[0:1, 0:256], eng="act")
            P.dma("pool", V(mod_s[l:l + 1, cb * 256:(cb + 1) * 256], mod_trk), mr)

    P.op("dve", lambda e: e.memset(fence_scr.t[:], 0.0), [], [V(None, ada_trk), fence_scr.all()])

    def vec_params(l):
        load_vec_cols(mod_s[l], 96, colv(l, O_MOD, 96), src_tr=mod_trk)
        load_vec_cols(ada_b_d[l], 96, modtmp)
        P.tt(colv(l, O_MOD, 96), colv(l, O_MOD, 96), modtmp, ALU.add)
        for o in (16, 32, 64, 80):
            P.ts(colv(l, O_MOD + o, 16), colv(l, O_MOD + o, 16), 1.0, ALU.add)
        load_vec_cols(ln1w_d[l], 16, colv(l, O_LN1W, 16))
        load_vec_cols(ln1b_d[l], 16, colv(l, O_LN1B, 16))
        load_vec_cols(ln2w_d[l], 16, colv(l, O_LN2W, 16))
        load_vec_cols(ln2b_d[l], 16, colv(l, O_LN2B, 16))
        load_vec_cols(pool_scale_d[l], 8, colv(l, O_PSC, 8))
        load_vec_cols(mu_d[l], 27, colv(l, O_MU, 27), nlast=32)
        load_vec_cols(w0_d[l], 8, colv(l, O_W0, 8))
        load_vec_cols(a0_d[l], 8, colv(l, O_A0, 8))
        load_vec_cols(kk_d[l], 8, colv(l, O_KK, 8))
        load_vec_cols(ka_d[l], 8, colv(l, O_KA, 8))
        P.ts(colv(l, O_KA1, 8), colv(l, O_KA, 8), -1.0, ALU.mult, 1.0, ALU.add)
        load_vec_cols(rk_d[l], 8, colv(l, O_RK, 8))
        load_vec_cols(lnw_d[l], 8, colv(l, O_LNW, 8))
        load_vec_cols(lnb_d[l], 8, colv(l, O_LNB, 8))
        if l >= 1:
            load_vec_cols(v0_d[l - 1], 8, colv(l, O_V0, 8))
        for j in range(4):
            load_vec_cols(conv_d[l, j], 24, colv(l, O_CONV + 24 * j, 24))
        load_vec_cols(gnw_d[l], 1, colv(l, O_GNW, 1))
        P.memset(g16[l].all(), 0.0)
        P.dma("sp", g16[l].all()[0:8, 0:1], dtb_d[l].rearrange("(p o) -> p o", o=1))
        P.dma("sp", g16[l].all()[0:8, 1:2], alog_d[l].rearrange("(p o) -> p o", o=1))
        P.act(g16[l].all()[:, 1:2], g16[l].all()[:, 1:2], AF.Exp)
        P.ts(g16[l].all()[:, 1:2], g16[l].all()[:, 1:2], -1.0, ALU.mult)

    ada_flat = ada_w_d.rearrange("l k n -> (l k n)")
    carve_off = [0]

    def conv_weight(name, src2d, K, tiles, carve=False):
        KTn = K // 128
        if carve:
            n32 = len(tiles) * 128 * KTn * 128 // 2
            scr = ada_flat[carve_off[0]:carve_off[0] + n32].bitcast(BF16).rearrange(
                "(t p k c) -> t p k c", t=len(tiles), p=128, k=KTn)
            carve_off[0] += n32
            assert carve_off[0] <= 2 * D * 6 * D
        else:
            scr = nc.dram_tensor(name, [len(tiles), 128, KTn, 128], BF16, kind="Internal").ap()
        scr_v = V(scr, [Trk(f"{name}_{t}") for t in range(len(tiles))])
        maxc = min(512, (5632 // KTn) // 128 * 128)
        i = 0
        while i < len(tiles):
            j = i
            ncols = tiles[i][1]
            while (j + 1 < len(tiles) and tiles[j][1] == 128 and tiles[j + 1][0] == tiles[j][0] + 128
                   and ncols + tiles[j + 1][1] <= maxc):
                j += 1
                ncols += tiles[j][1]
            c0 = tiles[i][0]
            k = sidx[0] % 2
            sidx[0] += 1
            sv = V(stg[k].ap[:, 0:KTn * ncols].rearrange("p (k c) -> p k c", k=KTn), stg[k].tr)
            bv = V(stb[k].ap[:, 0:KTn * ncols].rearrange("p (k c) -> p k c", k=KTn), stb[k].tr)
            P.dma("sp", sv, src2d[:, c0:c0 + ncols].rearrange("(k p) c -> p k c", p=128))
            e = ("dve", "pool", "act")[sidx[0] % 3]
            P.copy(bv, sv, eng=e)
            off = 0
            for t in range(i, j + 1):
                w = tiles[t][1]
                P.dma("pool", V(scr[t][:, :, 0:w], [scr_v.tr[t]]), bv[:, :, off:off + w],
                      reads=([V(None, ada_trk)] if carve else []))
                off += w
            i = j + 1
        return scr_v

    WIN, WUP, WDN, WOUT, WBR = [], [], [], [], []
    for l in range(NL):
        WIN.append(conv_weight(f"win{l}", w_in_d[l], D, IN_TILES, carve=(l > 0)))
        WBR.append([conv_weight(f"wbr{l}_{b}", wbr_d[b][l], C, D_TILES, carve=(l > 0)) for b in range(3)])
        WOUT.append(conv_weight(f"wout{l}", wout_d[l], D, D_TILES, carve=(l > 0)))
        WUP.append(conv_weight(f"wup{l}", up_d[l], D, UP_TILES, carve=(l > 0)))
        WDN.append(conv_weight(f"wdn{l}", dn_d[l], DFF, D_TILES, carve=(l > 0)))

    def small_w(dst, src2d, rows, ncols, row0=0, zero=False):
        k = sidx[0] % 2
        sidx[0] += 1
        sv = V(stg[k].ap[:, 0:ncols], stg[k].tr)
        if zero:
            P.memset(sv, 0.0)
        P.dma("sp", sv[row0:row0 + rows, :], src2d)
        if zero:
            P.copy(dst, sv, eng="dve")
        else:
            P.copy(dst, sv[row0:row0 + rows, :], eng="dve")

    for l in range(NL):
        small_w(W2z[l].all(), w2_d[l], 64, C, 0, zero=True)
        small_w(A2z[l].all(), a2_d[l], 64, C, 64, zero=True)
        small_w(G2a[l].all(), g2_d[l][0:128], 128, C)
        small_w(G2b[l].all(), g2_d[l][128:160], 32, C)
        k = sidx[0] % 2
        sidx[0] += 1
        sv = V(stg[k].ap[:, 0:2048].rearrange("p (g k d) -> p g k d", g=4, k=2), stg[k].tr)
        for g in range(4):
            P.dma("sp", sv[:, g], pool_w_d[l, g].rearrange("(k p) d -> p k d", p=128))
        P.copy(POOLW[l].all(), sv, eng="dve")
        P.dma("pool", V(small_s[l], small_trk[l]), SMALL.all())
    if NL > 1:
        k = sidx[0] % 2
        sidx[0] += 1
        sv = V(stg[k].ap[:, 0:256].rearrange("p (k c) -> p k c", k=8), stg[k].tr)
        P.dma("sp", sv, v1_d[0].rearrange("(k p) c -> p k c", p=128))
        P.copy(V1.all(), sv, eng="dve")
        small_w(V2.all(), v2_d[0], 32, C)

    for l in range(NL):
        vec_params(l)

    def layout_main():
        AR.reset()
        AR.views = []
        d = {}
        d["XS"] = AR.f32("XS", 128, 2 * D)
        return d

    def w_load(scr_v, t0, n, KTn):
        slot = wbuf[w_load.i % NWS]
        w_load.i += 1
        sv = V(slot.t[:, 0:n * KTn * 128].rearrange("p (t k c) -> p t k c", t=n, k=KTn), slot.trs)
        P.dma("sp", sv, V(scr_v.ap[t0:t0 + n].rearrange("t p k c -> p t k c"), scr_v.tr[t0:t0 + n]))
        return sv
    w_load.i = 0

    def proj(scr_v, KTn, t0, n, rhs, evac, widths=None, per_load=None):
        if per_load is None:
            per_load = max(1, WSLOT // (KTn * 128))
        i = 0
        while i < n:
            m = min(per_load, n - i)
            wv = w_load(scr_v, t0 + i, m, KTn)
            for q in range(m):
                w = 128 if widths is None else widths[i + q]
                b = nb()
                pv = b.all()[0:w, 0:TT]
                for kt in range(KTn):
                    P.mm(pv, wv[:, q, kt, 0:w], rhs(kt), start=(kt == 0), stop=(kt == KTn - 1))
                evac(i + q, pv, w)
            i += m

    def layer_norm(l, o_w, o_b, st1, st2, st3):
        b1 = nb()
        for kt in range(KT):
            P.mm(b1.all()[:, 0:TT], ones, xT[kt], start=(kt == 0), stop=(kt == KT - 1))
        b2 = nb()
        for kt in range(KT):
            sq = st3[kt % 2]
            P.act(sq, xT[kt], AF.Square)
            P.mm(b2.all()[:, 0:TT], ones, sq, start=(kt == 0), stop=(kt == KT - 1))
        mean, rstd = st1, st2
        P.act(mean, b1.all()[:, 0:TT], AF.Copy, scale=1.0 / D)
        P.act(rstd, b2.all()[:, 0:TT], AF.Copy, scale=1.0 / D)
        msq = st3[0]
        P.tt(msq, mean, mean, ALU.mult)
        P.tt(rstd, rstd, msq, ALU.subtract)
        P.act(rstd, rstd, AF.Sqrt, bias=LN_EPS)
        P.recip(rstd, rstd)
        for kt in range(KT):
            e = eng2()
            P.tt(xT[kt], xT[kt], mean, ALU.subtract, eng=e)
            P.tt(xT[kt], xT[kt], rstd, ALU.mult, eng=e)
            P.ts(xT[kt], xT[kt], colv(l, o_w + kt), ALU.mult, colv(l, o_b + kt), ALU.add, eng=e)

    def inv_chain(N0, A0, TTm, tmpN, tmpA):
        def blkv(v, c):
            return v[:, c * L:(c + 1) * L]
        P.tt(TTm, M_ID, N0, ALU.add)
        Nk, Ak = N0, A0
        for lev in range(1, 6):
            An = tmpA[lev % 2]
            b = nb()
            for c in range(NCH):
                P.mm(blkv(b.all()[0:64, 0:TT], c), blkv(Nk, c), blkv(Ak, c))
            P.copy(An, b.all()[0:64, 0:TT], eng="act")
            if lev < 5:
                Nn = tmpN[lev % 2]
                b = nb()
                for c in range(NCH):
                    P.mm(blkv(b.all()[0:64, 0:TT], c), blkv(Ak, c), blkv(Nk, c))
                P.copy(Nn, b.all()[0:64, 0:TT], eng="act")
            b = nb()
            for c in range(NCH):
                P.mm(blkv(b.all()[0:64, 0:TT], c), blkv(An, c), blkv(TTm, c))
            P.tt(TTm, b.all()[0:64, 0:TT], TTm, ALU.add)
            Ak = An
            if lev < 5:
                Nk = Nn

    prev_views = list(prologue_views)

    for ti in range(NTILES):
        tok0 = ti * TT
        AR.reset()
        AR.views = []
        XS = AR.f32("XS", 128, 2 * D)
        fence(prev_views, AR.views)
        prev_views = list(AR.views)
        XSv = V(XS.ap.rearrange("p (b d) -> p b d", b=2), XS.tr)
        P.dma("sp", XSv, x_d[tok0:tok0 + TT, :].rearrange("(b p) d -> p b d", p=128))
        for kt in range(KT):
            b = nb()
            for tb in range(2):
                P.mm(b.all()[:, tb * 128:(tb + 1) * 128], XSv[:, tb, kt * 128:(kt + 1) * 128], ident)
            P.copy(xT[kt], b.all()[:, 0:TT], eng="act")

        for l in range(NL if NL_main is None else NL_main):
            if NL > 1 or ti == 0:
                P.dma("sp", SMALL.all(), V(small_s[l], small_trk[l]))
            for kt in range(KT):
                P.ts(hT[kt], xT[kt], colv(l, O_MOD + 16 + kt), ALU.mult, colv(l, O_MOD + kt), ALU.add, eng=eng2())

            AR.reset()
            AR.views = []
            PP = AR.f32("PP", 128, 8 * (15 + TT))
            WA = AR.f32("WA", 128, 8 * (15 + TT))
            WB = AR.f32("WB", 128, 8 * (15 + TT))
            PLD = AR.bf16("PLD", 128, 8 * TT)
            PT16 = AR.f32("PT16", 128, 2 * 16)
            fence(prev_views, AR.views)
            prev_views = list(AR.views)
            W_ = 15 + TT

            def v3(v):
                return V(v.ap.rearrange("p (g w) -> p g w", g=8), v.tr)
            PP3, WA3, WB3 = v3(PP), v3(WA), v3(WB)
            PLD3 = V(PLD.ap.rearrange("p (g w) -> p g w", g=8), PLD.tr)
            P.copy(PP3[:, :, 0:15], pooltail[l].all())

            def ev_pool(i, pv, w):
                P.copy(PP3[:, i, 15:15 + TT], pv, eng="act")
            proj(WIN[l], KT, T_POOL, 8, lambda kt: hT[kt], ev_pool)
            P.copy(pooltail[l].all(), PP3[:, :, TT:TT + 15], eng="pool")
            P.tt(WA3[:, :, 1:W_], PP3[:, :, 1:W_], PP3[:, :, 0:W_ - 1], ALU.add)
            P.tt(WB3[:, 2:8, 3:W_], WA3[:, 2:8, 3:W_], WA3[:, 2:8, 1:W_ - 2], ALU.add, eng="pool")
            P.tt(WA3[:, 4:8, 7:W_], WB3[:, 4:8, 7:W_], WB3[:, 4:8, 3:W_ - 4], ALU.add)
            P.tt(WB3[:, 6:8, 15:W_], WA3[:, 6:8, 15:W_], WA3[:, 6:8, 7:W_ - 8], ALU.add, eng="pool")
            srcs = [WA3, WB3, WA3, WB3]
            for g, win in enumerate((2, 4, 8, 16)):
                sl_ = slice(2 * g, 2 * g + 2)
                P.stt(PLD3[:, sl_, :], srcs[g][:, sl_, 15:W_], 1.0 / win, PP3[:, sl_, 15:W_], ALU.mult, ALU.subtract)
                if ti == 0:
                    icv = V(c128.t[:, 388 + TT + g * 16:388 + TT + (g + 1) * 16].unsqueeze(1).to_broadcast([128, 2, 16]),
                            c128.trs)
                    t16 = V(PT16.ap.rearrange("p (g w) -> p g w", g=2), PT16.tr)
                    P.tt(t16, srcs[g][:, sl_, 15:31], icv, ALU.mult)
                    P.tt(PLD3[:, sl_, 0:16], t16, PP3[:, sl_, 15:31], ALU.subtract)
            for g in range(4):
                for dt in range(2):
                    b = nb()
                    for ct in range(2):
                        P.mm(b.all()[:, 0:TT], V(POOLW[l].t[:, g, ct, dt * 128:(dt + 1) * 128], POOLW[l].trs),
                             PLD3[:, 2 * g + ct, :], start=(ct == 0), stop=(ct == 1))
                    P.ts(yb[0][2 * g + dt], b.all()[:, 0:TT], colv(l, O_PSC + 2 * g + dt), ALU.mult)

            AR.reset()
            AR.views = []
            f = lambda n, p=128, w=TT: AR.f32(n, p, w)
            PS_ = AR.f32("PS", 128, 1 + TT)
            DIF = f("DIF")
            XSL = f("XSL")
            L1 = AR.bf16("L1", 128, TT)
            L2 = AR.bf16("L2", 128, TT)
            L3 = AR.bf16("L3", 32, TT)
            VX = AR.f32("VX", 128, 8 * TT) if l > 0 else None
            VXB = AR.bf16("VXB", 128, 8 * TT) if l > 0 else None
            VVS = AR.bf16("VVS", 32, TT) if l > 0 else None
            R_, K0, SIG, AA, GG, KKt, KM, BON, CUM, T1, T2, T3 = [f(n) for n in
                ("R", "K0", "SIG", "AA", "GG", "KK", "KM", "BON", "CUM", "T1", "T2", "T3")]
            AZ = [f("AZ0"), f("AZ1")]
            BZ = [f("BZ0"), f("BZ1")]
            KZ = [f("KZ0"), f("KZ1")]
            RZ = [f("RZ0"), f("RZ1")]
            BTM = AR.f32("BTM", 64, NCH * 128)
            KTM = AR.f32("KTM", 64, NCH * 128)
            VTM = AR.f32("VTM", 64, NCH * 128)
            g64 = lambda n: AR.f32(n, 64, TT)
            AAK = [g64("AAK0"), g64("AAK1")]
            ARB = [g64("ARB0"), g64("ARB1")]
            ARK = [g64("ARK0"), g64("ARK1")]
            TTM = [g64("TTM0"), g64("TTM1")]
            N0, A0 = g64("N0"), g64("A0")
            tmpN = [g64("tN0"), g64("tN1")]
            tmpA = [g64("tA0"), g64("tA1")]
            W0S = AR.f32("W0S", 64, 64)
            US = [AR.f32("US0", 64, 64), AR.f32("US1", 64, 64)]
            HT_ = AR.f32("HT", 128, 64)
            fence(prev_views, AR.views)
            prev_views = list(AR.views)

            def shift_xs(pv, w, tix, dst):
                P.copy(PS_[0:w, 1:1 + TT], pv, eng="act")
                P.copy(PS_[0:w, 0:1], V(shtail[l].t[0:w, tix:tix + 1], shtail[l].trs), eng="pool")
                P.tt(DIF[0:w, :], PS_[0:w, 0:TT], PS_[0:w, 1:1 + TT], ALU.subtract)
                P.copy(V(shtail[l].t[0:w, tix:tix + 1], shtail[l].trs), PS_[0:w, TT:TT + 1], eng="pool")
                P.stt(dst, DIF[0:w, :], V(cols[l].t[0:w, O_MU + tix:O_MU + tix + 1], cols[l].trs),
                      PS_[0:w, 1:1 + TT], ALU.mult, ALU.add)

            def ev_lora(i, pv, w):
                shift_xs(pv, w, 24 + i, XSL[0:w, :])
                if i == 0:
                    P.act(L1[0:64, :], XSL[0:64, :], AF.Tanh)
                    P.copy(L1[64:128, :], XSL[64:128, :], eng="pool")
                elif i == 1:
                    P.act(L2, XSL, AF.Sigmoid)
                else:
                    P.act(L3, XSL[0:32, :], AF.Sigmoid)
            proj(WIN[l], KT, T_LO, 3, lambda kt: hT[kt], ev_lora, widths=[128, 128, 32])

            if l > 0:
                VX3 = V(VX.ap.rearrange("p (j w) -> p j w", j=8), VX.tr)
                VXB3 = V(VXB.ap.rearrange("p (j w) -> p j w", j=8), VXB.tr)

                def ev_v(i, pv, w):
                    shift_xs(pv, w, 16 + i, VX3[:, i, :])
                    P.copy(VXB3[:, i, :], VX3[:, i, :], eng="pool")
                proj(WIN[l], KT, T_V, 8, lambda kt: hT[kt], ev_v)
                b = nb()
                for kt in range(8):
                    P.mm(b.all()[0:32, 0:TT], V(V1.t[:, kt, :], V1.trs), VXB3[:, kt, :], start=(kt == 0), stop=(kt == 7))
                P.copy(VVS, b.all()[0:32, 0:TT], eng="act")

            for j in range(8):
                jc = slice(j * 128, (j + 1) * 128)
                Vt = vfirst[j] if l == 0 else VX3[:, j, :]
                proj(WIN[l], KT, T_R + j, 1, lambda kt: hT[kt], lambda i, pv, w: shift_xs(pv, w, j, R_))
                proj(WIN[l], KT, T_K + j, 1, lambda kt: hT[kt], lambda i, pv, w: shift_xs(pv, w, 8 + j, K0))
                if l == 0:
                    proj(WIN[l], KT, T_V + j, 1, lambda kt: hT[kt], lambda i, pv, w: shift_xs(pv, w, 16 + j, Vt))
                b = nb()
                P.mm(b.all()[:, 0:TT], V(W2z[l].t[:, jc], W2z[l].trs), L1)
                P.act(SIG, b.all()[:, 0:TT], AF.Sigmoid, bias=colv(l, O_W0 + j))
                b = nb()
                P.mm(b.all()[:, 0:TT], V(A2z[l].t[:, jc], A2z[l].trs), L1)
                P.act(AA, b.all()[:, 0:TT], AF.Sigmoid, bias=colv(l, O_A0 + j))
                b = nb()
                P.mm(b.all()[:, 0:TT], V(G2a[l].t[:, jc], G2a[l].trs), L2, start=True, stop=False)
                P.mm(b.all()[:, 0:TT], V(G2b[l].t[:, jc], G2b[l].trs), L3, start=False, stop=True)
                P.copy(GG, b.all()[:, 0:TT], eng="act")
                if l > 0:
                    b = nb()
                    P.mm(b.all()[:, 0:TT], V(V2.t[:, jc], V2.trs), VVS)
                    P.act(T1, b.all()[:, 0:TT], AF.Sigmoid, bias=colv(l, O_V0 + j))
                    P.tt(T2, vfirst[j], Vt, ALU.subtract)
                    P.tt(T2, T2, T1, ALU.mult)
                    P.tt(Vt, Vt, T2, ALU.add)
                P.ts(KKt, K0, colv(l, O_KK + j), ALU.mult)
                P.tt(T1, KKt, KKt, ALU.mult, eng="pool")
                b = nb()
                P.mm(b.all()[:, 0:TT], blk, T1)
                P.act(T1, b.all()[:, 0:TT], AF.Sqrt)
                P.ts(T1, T1, 1e-12, ALU.max)
                P.recip(T1, T1)
                P.tt(KKt, KKt, T1, ALU.mult)
                P.ts(T2, AA, colv(l, O_KA + j), ALU.mult, colv(l, O_KA1 + j), ALU.add, eng="pool")
                P.tt(KM, K0, T2, ALU.mult, eng="pool")
                P.stt(T2, R_, colv(l, O_RK + j), KM, ALU.mult, ALU.mult)
                b = nb()
                P.mm(b.all()[:, 0:TT], blk, T2)
                P.tt(BON, b.all()[:, 0:TT], Vt, ALU.mult)
                P.scan(CUM, rmask, SIG)
                if dbg is not None and dbg == (ti, l) and j == 0:
                    P.dma("pool", dbgx_d[9], CUM)
                P.tt(T1, CUM, SIG, ALU.subtract, eng="pool")
                P.act(T1, T1, AF.Exp, scale=-C0)
                P.act(T2, CUM, AF.Exp, scale=-C0)
                P.act(T3, CUM, AF.Exp, scale=C0)
                P.tt(K0, KKt, AA, ALU.mult, eng="pool")
                for h in range(2):
                    P.stt(AZ[h], KKt, mcol(2 + h), T1, ALU.mult, ALU.mult, eng=eng2())
                    P.stt(BZ[h], K0, mcol(h), T3, ALU.mult, ALU.mult, eng=eng2())
                    P.stt(KZ[h], KM, mcol(h), T3, ALU.mult, ALU.mult, eng=eng2())
                    P.stt(RZ[h], R_, mcol(h), T2, ALU.mult, ALU.mult, eng=eng2())
                for (dst, srcs_) in ((BTM, BZ), (KTM, KZ), (VTM, [Vt])):
                    b = nb()
                    for c in range(NCH):
                        for si, s in enumerate(srcs_):
                            P.mm(b.all()[0:64, c * 128:(c + 1) * 128], s[:, c * L:(c + 1) * L], ident,
                                 start=(si == 0), stop=(si == len(srcs_) - 1))
                    P.copy(dst, b.all()[0:64, 0:NCH * 128], eng="act")
                for h in range(2):
                    def amat(dst, lh, rh, mask):
                        b = nb()
                        for c in range(NCH):
                            cs = slice(c * L, (c + 1) * L)
                            P.mm(b.all()[0:64, cs], lh[:, cs], rh[:, cs])
                        P.tt(dst, b.all()[0:64, 0:TT], mask, ALU.mult)
                    amat(N0, BZ[h], AZ[h], M_SU)
                    amat(A0, AZ[h], BZ[h], M_SL)
                    amat(AAK[h], KZ[h], AZ[h], M_SU)
                    amat(ARB[h], BZ[h], RZ[h], M_U)
                    amat(ARK[h], KZ[h], RZ[h], M_U)
                    inv_chain(N0, A0, TTM[h], tmpN, tmpA)
                Hp = rwH[l][j]
                for c in range(NCH):
                    cs = slice(c * L, (c + 1) * L)
                    for h in range(2):
                        hs = slice(c * 128 + h * 64, c * 128 + (h + 1) * 64)
                        hp = slice(h * 64, (h + 1) * 64)
                        b = nb()
                        P.mm(b.all()[0:64, 0:64], AZ[h][:, cs], Hp, start=True, stop=False)
                        P.mm(b.all()[0:64, 0:64], AAK[h][:, cs], VTM[:, hs], start=False, stop=True)
                        P.copy(W0S, b.all()[0:64, 0:64], eng="act")
                        b = nb()
                        P.mm(b.all()[0:64, 0:64], TTM[h][:, cs], W0S)
                        P.copy(US[h], b.all()[0:64, 0:64], eng="act")
                        yv = BANK_Y.all()[hp, cs]
                        P.mm(yv, Hp, RZ[h][:, cs], start=True, stop=False)
                        P.mm(yv, US[h], ARB[h][:, cs], start=False, stop=False)
                        P.mm(yv, VTM[:, hs], ARK[h][:, cs], start=False, stop=True)
                        hv = BANK_H.all()[hp, 0:64]
                        P.mm(hv, BTM[:, hs], US[h], start=True, stop=False)
                        P.mm(hv, KTM[:, hs], VTM[:, hs], start=False, stop=True)
                    P.tt(HT_, BANK_H.all()[:, 0:64], Hp, ALU.add)
                    P.ts(Hp, HT_, T2[:, c * L + L - 1:c * L + L], ALU.mult)
                if dbg is not None and dbg == (ti, l) and j == 0:
                    for qi, vv_ in enumerate((SIG, AA, GG, R_, KKt, CUM, BON, KM, AZ[0], BZ[1], KZ[0], rmask)):
                        if qi == 9:
                            continue
                        P.dma("pool", dbgx_d[qi], vv_)
                Y = T1
                P.copy(Y, BANK_Y.all()[:, 0:TT], eng="act")
                b1 = nb()
                P.mm(b1.all()[:, 0:TT], blk, Y)
                P.tt(T3, Y, Y, ALU.mult, eng="pool")
                b2 = nb()
                P.mm(b2.all()[:, 0:TT], blk, T3)
                P.act(T2, b1.all()[:, 0:TT], AF.Copy, scale=1.0 / 64)
                P.act(T3, b2.all()[:, 0:TT], AF.Copy, scale=1.0 / 64)
                P.tt(KM, T2, T2, ALU.mult)
                P.tt(T3, T3, KM, ALU.subtract)
                P.act(T3, T3, AF.Sqrt, bias=64 * 1e-5)
                P.recip(T3, T3)
                P.tt(Y, Y, T2, ALU.subtract)
                P.tt(Y, Y, T3, ALU.mult)
                P.ts(Y, Y, colv(l, O_LNW + j), ALU.mult, colv(l, O_LNB + j), ALU.add)
                P.tt(Y, Y, BON, ALU.add)
                P.tt(yb[1][j], Y, GG, ALU.mult)

            AR.reset()
            AR.views = []
            AB = AR.f32("AB", 16, TT)
            SP = AR.f32("SP", 16, TT)
            SG = AR.f32("SG", 16, TT)
            G16 = AR.f32("G16", 16, TT)
            SEL1 = AR.f32("SEL1", 16, TT)
            SEL2 = AR.f32("SEL2", 16, TT)
            GT = AR.f32("GT", 64, NCH * 16)
            BT = AR.f32("BT", 64, NCH * 16)
            ED = AR.f32("ED", 64, NCH * 16)
            EG = AR.f32("EG", 64, NCH * 16)
            BEG = AR.f32("BEG", 64, NCH * 16)
            P3 = AR.f32("P3", 128, 3 + TT)
            ACC = f("ACC")
            Q_, K_, V_, Z_, KB, QG, GB, EGB, U1, U2 = [f(n) for n in ("Q", "K", "Vg", "Z", "KB", "QG", "GB", "EGB", "U1", "U2")]
            KTMg = AR.f32("KTMg", 64, NCH * 128)
            VTMg = AR.f32("VTMg", 64, NCH * 128)
            KBG = AR.f32("KBG", 64, NCH * 128)
            VB = AR.f32("VB", 64, NCH * 128)
            KD = AR.f32("KD", 64, NCH * 128)
            USg = AR.f32("USg", 64, NCH * 128)
            WT = f("WT")
            D1 = g64("D1")
            GSU, GU, GSL = g64("GSU"), g64("GU"), g64("GSL")
            N0g, A0g, AQK, TTg = g64("N0g"), g64("A0g"), g64("AQK"), g64("TTg")
            tmpNg = [g64("tNg0"), g64("tNg1")]
            tmpAg = [g64("tAg0"), g64("tAg1")]
            VN = AR.f32("VN", 64, 128)
            fence(prev_views, AR.views)
            prev_views = list(AR.views)

            def ev_ab(i, pv, w):
                P.copy(AB, pv, eng="act")
            proj(WIN[l], KT, T_GAB, 1, lambda kt: hT[kt], ev_ab, widths=[16])
            P.act(SP, AB, AF.Exp, bias=V(g16[l].t[:, 0:1], g16[l].trs))
            P.act(SP, SP, AF.Ln, bias=1.0)
            P.ts(SP, SP, V(g16[l].t[:, 1:2], g16[l].trs), ALU.mult)
            P.act(SG, AB, AF.Sigmoid)
            P.scan(G16, rmask[0:16, :], SP)
            b = nb()
            b2 = nb()
            for c in range(NCH):
                cs = slice(c * L, (c + 1) * L)
                P.mm(b.all()[0:64, c * 16:(c + 1) * 16], G16[:, cs], ident[0:16, 0:16])
                P.mm(b2.all()[0:64, c * 16:(c + 1) * 16], SG[:, cs], ident[0:16, 0:16])
            P.copy(GT, b.all()[0:64, 0:NCH * 16], eng="act")
            P.copy(BT, b2.all()[0:64, 0:NCH * 16], eng="act")
            b = nb()
            P.mm(b.all()[0:64, 0:NCH * 16], E63, GT)
            P.tt(ED, b.all()[0:64, 0:NCH * 16], GT, ALU.subtract)
            P.act(ED, ED, AF.Exp)
            P.act(EG, GT, AF.Exp)
            GT3 = V(GT.ap.rearrange("p (c s) -> p c s", c=NCH), GT.tr)
            BT3 = V(BT.ap.rearrange("p (c s) -> p c s", c=NCH), BT.tr)
            ED3 = V(ED.ap.rearrange("p (c s) -> p c s", c=NCH), ED.tr)
            EG3 = V(EG.ap.rearrange("p (c s) -> p c s", c=NCH), EG.tr)
            BEG3 = V(BEG.ap.rearrange("p (c s) -> p c s", c=NCH), BEG.tr)
            P.tt(BEG3[:, :, 0:8], BT3[:, :, 8:16], EG3[:, :, 0:8], ALU.mult)

            def conv_silu(pv, tix, dst):
                P.copy(P3[:, 3:3 + TT], pv, eng="act")
                P.copy(P3[:, 0:3], V(cvtail[l].t[:, tix, :], cvtail[l].trs), eng="pool")
                P.ts(ACC, P3[:, 0:TT], colv(l, O_CONV + tix), ALU.mult)
                for jj in range(1, 4):
                    P.stt(ACC, P3[:, jj:jj + TT], colv(l, O_CONV + 24 * jj + tix), ACC, ALU.mult, ALU.add)
                P.copy(V(cvtail[l].t[:, tix, :], cvtail[l].trs), P3[:, TT:TT + 3], eng="pool")
                P.act(dst, ACC, AF.Silu)

            def l2n(x_, scale):
                P.tt(U1, x_, x_, ALU.mult, eng="pool")
                bb = nb()
                P.mm(bb.all()[:, 0:TT], ones, U1)
                P.act(U1, bb.all()[:, 0:TT], AF.Sqrt, bias=1e-6)
                P.recip(U1, U1)
                P.stt(x_, x_, scale, U1, ALU.mult, ALU.mult)

            def r3(v, w):
                return V(v.ap.rearrange("p (c s) -> p c s", c=NCH), v.tr)

            for h in range(8):
                proj(WIN[l], KT, T_GQ + h, 1, lambda kt: hT[kt], lambda i, pv, w: conv_silu(pv, h, Q_))
                proj(WIN[l], KT, T_GK + h, 1, lambda kt: hT[kt], lambda i, pv, w: conv_silu(pv, 8 + h, K_))
                proj(WIN[l], KT, T_GV + h, 1, lambda kt: hT[kt], lambda i, pv, w: conv_silu(pv, 16 + h, V_))
                proj(WIN[l], KT, T_GZ + h, 1, lambda kt: hT[kt], lambda i, pv, w: P.act(Z_, pv, AF.Silu))
                l2n(Q_, float(128 ** -0.5))
                l2n(K_, 1.0)
                P.ts(SEL1, G16, ident[0:16, h:h + 1], ALU.mult)
                b = nb()
                P.mm(b.all()[:, 0:TT], ones[0:16, :], SEL1)
                P.copy(GB, b.all()[:, 0:TT], eng="act")
                P.act(EGB, GB, AF.Exp)
                P.ts(SEL2, SG, ident[0:16, 8 + h:9 + h], ALU.mult)
                b = nb()
                P.mm(b.all()[:, 0:TT], ones[0:16, :], SEL2)
                P.tt(KB, K_, b.all()[:, 0:TT], ALU.mult)
                P.tt(QG, Q_, EGB, ALU.mult, eng="pool")
                for (dst, s) in ((KTMg, K_), (VTMg, V_)):
                    b = nb()
                    for c in range(NCH):
                        P.mm(b.all()[0:64, c * 128:(c + 1) * 128], s[:, c * L:(c + 1) * L], ident)
                    P.copy(dst, b.all()[0:64, 0:NCH * 128], eng="act")
                K3, V3 = r3(KTMg, 128), r3(VTMg, 128)

                def bc(v3_, col):
                    return V(v3_.ap[:, :, col:col + 1].to_broadcast([64, NCH, 128]), v3_.tr)
                P.tt(r3(KBG, 128), K3, bc(BEG3, h), ALU.mult)
                P.tt(r3(VB, 128), V3, bc(BT3, 8 + h), ALU.mult, eng="pool")
                P.tt(r3(KD, 128), K3, bc(ED3, h), ALU.mult)
                GBr = V(GB.ap[0:64, :].rearrange("p (c s) -> p c s", c=NCH), GB.tr)
                gtb = V(GT3.ap[:, :, h:h + 1].to_broadcast([64, NCH, L]), GT.tr)
                P.tt(r3(D1, L), GBr, gtb, ALU.subtract)
                P.tt(GSU, D1, GB_SU, ALU.add)
                P.act(GSU, GSU, AF.Exp)
                P.tt(GU, D1, GB_U, ALU.add, eng="pool")
                P.act(GU, GU, AF.Exp)
                P.stt(GSL, D1, -1.0, GB_SL, ALU.mult, ALU.add)
                P.act(GSL, GSL, AF.Exp)

                def gmat(dst, lh, rh, gam, neg):
                    b = nb()
                    for c in range(NCH):
                        cs = slice(c * L, (c + 1) * L)
                        P.mm(b.all()[0:64, cs], lh[:, cs], rh[:, cs])
                    if neg:
                        P.stt(dst, b.all()[0:64, 0:TT], -1.0, gam, ALU.mult, ALU.mult)
                    else:
                        P.tt(dst, b.all()[0:64, 0:TT], gam, ALU.mult)
                gmat(N0g, K_, KB, GSU, True)
                gmat(A0g, KB, K_, GSL, True)
                gmat(AQK, K_, Q_, GU, False)
                inv_chain(N0g, A0g, TTg, tmpNg, tmpAg)
                b = nb()
                for c in range(NCH):
                    P.mm(b.all()[0:64, c * 128:(c + 1) * 128], TTg[:, c * L:(c + 1) * L], VB[:, c * 128:(c + 1) * 128])
                P.copy(USg, b.all()[0:64, 0:NCH * 128], eng="act")
                b = nb()
                for c in range(NCH):
                    P.mm(b.all()[:, c * L:(c + 1) * L], KBG[:, c * 128:(c + 1) * 128], TTg[:, c * L:(c + 1) * L])
                P.copy(WT, b.all()[:, 0:TT], eng="act")
                S = gdS[l][h]
                for c in range(NCH):
                    cs = slice(c * L, (c + 1) * L)
                    cw = slice(c * 128, (c + 1) * 128)
                    b = nb()
                    P.mm(b.all()[0:64, 0:128], WT[:, cs], S)
                    P.tt(VN, USg[:, cw], b.all()[0:64, 0:128], ALU.subtract)
                    ov = BANK_Y.all()[:, cs]
                    P.mm(ov, S, QG[:, cs], start=True, stop=False)
                    P.mm(ov, VN, AQK[:, cs], start=False, stop=True)
                    P.mm(BANK_H.all()[:, 0:128], KD[:, cw], VN)
                    P.stt(S, S, EGB[:, c * L + L - 1:c * L + L], BANK_H.all()[:, 0:128], ALU.mult, ALU.add)
                O = U2
                P.copy(O, BANK_Y.all()[:, 0:TT], eng="act")
                P.tt(U1, O, O, ALU.mult, eng="pool")
                b = nb()
                P.mm(b.all()[:, 0:TT], ones, U1)
                P.act(U1, b.all()[:, 0:TT], AF.Sqrt, bias=1e-6, scale=1.0 / 128)
                P.recip(U1, U1)
                P.tt(O, O, U1, ALU.mult)
                P.stt(yb[2][h], O, colv(l, O_GNW), Z_, ALU.mult, ALU.mult)

            if dbg is not None and dbg == (ti, l):
                P.dma("pool", dbgh_d, hT.all())
                P.dma("pool", dbgc_d, cols[l].all())
                for br in range(3):
                    P.dma("pool", dbg_d[br], yb[br].all())
            AR.reset()
            AR.views = []
            ACCM = AR.f32("ACCM", 128, KT * TT)
            GS = [AR.f32("GS0", 128, TT), AR.f32("GS1", 128, TT)]
            TM = [AR.f32("TM0", 128, TT), AR.f32("TM1", 128, TT)]
            ST1, ST2 = AR.f32("ST1", 128, TT), AR.f32("ST2", 128, TT)
            ST3 = [AR.f32("ST3a", 128, TT), AR.f32("ST3b", 128, TT)]
            HID = AR.bf16("HID", 128, KTF * TT)
            fence(prev_views, AR.views)
            prev_views = list(AR.views)
            ACC3 = V(ACCM.ap.rearrange("p (k w) -> p k w", k=KT), ACCM.tr)
            HID3 = V(HID.ap.rearrange("p (k w) -> p k w", k=KTF), HID.tr)
            for br in range(3):
                for dt0 in range(0, 16, 3):
                    n = min(3, 16 - dt0)
                    wv_g = w_load(WIN[l], T_GATE + br * 16 + dt0, n, KT)
                    wv_b = w_load(WBR[l][br], dt0, n, 8)
                    for q in range(n):
                        dt = dt0 + q
                        b = nb()
                        for kt in range(KT):
                            P.mm(b.all()[:, 0:TT], wv_g[:, q, kt, :], hT[kt], start=(kt == 0), stop=(kt == KT - 1))
                        g_ = GS[dt % 2]
                        P.act(g_, b.all()[:, 0:TT], AF.Sigmoid)
                        b = nb()
                        for kt in range(8):
                            P.mm(b.all()[:, 0:TT], wv_b[:, q, kt, :], yb[br][kt], start=(kt == 0), stop=(kt == 7))
                        if br == 0:
                            P.tt(ACC3[:, dt, :], g_, b.all()[:, 0:TT], ALU.mult)
                        elif br == 1:
                            t_ = TM[dt % 2]
                            P.tt(t_, g_, b.all()[:, 0:TT], ALU.mult)
                            P.tt(ACC3[:, dt, :], ACC3[:, dt, :], t_, ALU.add, eng="pool")
                        else:
                            t_ = TM[dt % 2]
                            P.tt(t_, g_, b.all()[:, 0:TT], ALU.mult)
                            P.tt(merged[dt], ACC3[:, dt, :], t_, ALU.add, eng="pool")

            def ev_res(o_gate):
                def ev(i, pv, w):
                    t_ = TM[i % 2]
                    P.ts(t_, pv, colv(l, O_MOD + o_gate + i), ALU.mult)
                    P.stt(xT[i], xT[i], DN_ALPHA, t_, ALU.mult, ALU.add, eng="pool")
                return ev
            proj(WOUT[l], KT, 0, 16, lambda kt: merged[kt], ev_res(32))
            layer_norm(l, O_LN1W, O_LN1B, ST1, ST2, ST3)

            for kt in range(KT):
                P.ts(hT[kt], xT[kt], colv(l, O_MOD + 64 + kt), ALU.mult, colv(l, O_MOD + 48 + kt), ALU.add, eng=eng2())

            def ev_up(i, pv, w):
                if i % 2 == 0:
                    P.act(GS[(i // 2) % 2], pv, AF.Silu)
                else:
                    P.tt(HID3[:, i // 2, :], GS[(i // 2) % 2], pv, ALU.mult)
            proj(WUP[l], KT, 0, 2 * KTF, lambda kt: hT[kt], ev_up, per_load=2)
            proj(WDN[l], KTF, 0, 16, lambda kt: HID3[:, kt, :], ev_res(80))
            layer_norm(l, O_LN2W, O_LN2B, ST1, ST2, ST3)

        AR.reset()
        AR.views = []
        OS = AR.f32("OS", 128, 2 * D)
        fence(prev_views, AR.views)
        prev_views = list(AR.views)
        OSv = V(OS.ap.rearrange("p (b d) -> p b d", b=2), OS.tr)
        for tb in range(2):
            for k4 in range(4):
                b = nb()
                for q in range(4):
                    kt = k4 * 4 + q
                    P.mm(b.all()[:, q * 128:(q + 1) * 128], xT[kt][:, tb * 128:(tb + 1) * 128], ident)
                P.copy(OSv[:, tb, k4 * 512:(k4 + 1) * 512], b.all(), eng="act")
        P.dma("pool", out_d[tok0:tok0 + TT, :].rearrange("(b p) d -> p b d", p=128), OSv)

    P.emit()
    P.close()
    return nc, P


_CONSTS = None


def kernel(**inputs):
    global _CONSTS
    x = np.asarray(inputs["x"], np.float32)
    B, T, _ = x.shape
    nc, _ = build(T, 2)
    c128, c64 = make_consts()
    shared = {k: np.ascontiguousarray(np.asarray(v, np.float32)) for k, v in inputs.items() if k not in ("x", "c")}
    shared["rwkv_r_k"] = shared["rwkv_r_k"].reshape(2, C)
    shared.update(c128=c128, c64=c64)
    in_maps = []
    for core in range(8):
        b = core % B
        m = dict(shared)
        m["x"] = np.ascontiguousarray(x[b])
        m["c"] = np.ascontiguousarray(np.asarray(inputs["c"], np.float32)[b])
        in_maps.append(m)
    res = run_bass_kernel_spmd(nc, in_maps, core_ids=list(range(8)))
    return np.stack([res.results[b]["out"] for b in range(B)], axis=0).astype(np.float32)
```

```python
import numpy as np
from contextlib import ExitStack
import concourse.bass as bass
import concourse.mybir as mybir
from concourse.bass_utils import run_bass_kernel_spmd

F32 = mybir.dt.float32
BF16 = mybir.dt.bfloat16
AF = mybir.ActivationFunctionType
ALU = mybir.AluOpType

EPOCH = 16000
D = 2048
KT = 16
C = 1024
TT = 256
L = 64
NCH = TT // L
DFF = 5632
KTF = DFF // 128
N_IN = 14640
DN_ALPHA = 4.0 ** 0.25
LN_EPS = 1e-5
C0 = float(np.exp(-0.5))
R0 = 1024
G0 = 4384
Q0 = 8496


class Trk:
    __slots__ = ("name", "w", "r")

    def __init__(self, name):
        self.name = name
        self.w = None
        self.r = []


class V:
    __slots__ = ("ap", "tr")

    def __init__(self, ap, tr):
        self.ap = ap
        self.tr = tr

    def __getitem__(self, key):
        return V(self.ap[key], self.tr)


class Buf:
    def __init__(self, t, trs, split):
        self.t = t
        self.trs = trs
        self.split = split

    def __getitem__(self, i):
        assert self.split
        return V(self.t[:, i], [self.trs[i]])

    def all(self):
        return V(self.t[:], list(self.trs))


class Prog:
    ENGS = ("pe", "act", "dve", "pool", "sp")

    def __init__(self, nc, n_dma_sems=16):
        self.nc = nc
        self.es = ExitStack()
        self.ops = []
        self.n_dma_sems = n_dma_sems

    def sb(self, name, shape, dtype, split=False):
        t = self.es.enter_context(self.nc.sbuf_tensor("s_" + name, list(shape), dtype))
        n = shape[1] if split else 1
        return Buf(t, [Trk(f"{name}{i}") for i in range(n)], split)

    def ps(self, name, shape, dtype=F32):
        t = self.es.enter_context(self.nc.psum_tensor("p_" + name, list(shape), dtype))
        return Buf(t, [Trk(name)], False)

    def _rec(self, eng, fn, reads, writes, dma):
        oid = len(self.ops)
        deps = set()
        for v in reads:
            for t in v.tr:
                if t.w is not None:
                    deps.add(t.w)
                t.r.append(oid)
        for v in writes:
            for t in v.tr:
                if t.w is not None:
                    deps.add(t.w)
                deps.update(x for x in t.r if x != oid)
                t.w = oid
                t.r = []
        self.ops.append(dict(eng=eng, fn=fn, deps=sorted(deps), dma=dma))
        return oid

    def op(self, eng, fn, reads=(), writes=()):
        return self._rec(eng, fn, reads, writes, False)

    def dma(self, q, out, in_, reads=(), writes=()):
        rd = list(reads)
        wr = list(writes)
        oap, iap = out, in_
        if isinstance(out, V):
            wr.append(out)
            oap = out.ap
        if isinstance(in_, V):
            rd.append(in_)
            iap = in_.ap
        return self._rec(q, lambda e: e.dma_start(out=oap, in_=iap), rd, wr, True)

    def mm(self, out, lhsT, rhs, start=True, stop=True):
        return self.op("pe", lambda e: e.matmul(out.ap, lhsT.ap, rhs.ap, start=start, stop=stop),
                       [lhsT, rhs] + ([] if start else [out]), [out])

    def act(self, out, in_, func, bias=None, scale=None):
        kw = {}
        rd = [in_]
        if bias is not None:
            if isinstance(bias, V):
                rd.append(bias)
                kw["bias"] = bias.ap
            else:
                kw["bias"] = bias
        if scale is not None:
            if isinstance(scale, V):
                rd.append(scale)
                kw["scale"] = scale.ap
            else:
                kw["scale"] = scale
        return self.op("act", lambda e: e.activation(out=out.ap, in_=in_.ap, func=func, **kw), rd, [out])

    def tt(self, out, in0, in1, op, eng="dve"):
        return self.op(eng, lambda e: e.tensor_tensor(out=out.ap, in0=in0.ap, in1=in1.ap, op=op),
                       [in0, in1], [out])

    def ts(self, out, in0, s1, op0, s2=None, op1=None, eng="dve"):
        rd = [in0]
        a1 = s1
        if isinstance(s1, V):
            rd.append(s1)
            a1 = s1.ap
        a2 = s2
        if isinstance(s2, V):
            rd.append(s2)
            a2 = s2.ap
        if op1 is None:
            return self.op(eng, lambda e: e.tensor_scalar(out=out.ap, in0=in0.ap, scalar1=a1, scalar2=None,
                                                          op0=op0), rd, [out])
        return self.op(eng, lambda e: e.tensor_scalar(out=out.ap, in0=in0.ap, scalar1=a1, scalar2=a2,
                                                      op0=op0, op1=op1), rd, [out])

    def stt(self, out, in0, s, in1, op0, op1, eng="dve"):
        eng = "dve"
        rd = [in0, in1]
        a = s
        if isinstance(s, V):
            rd.append(s)
            a = s.ap
        return self.op(eng, lambda e: e.scalar_tensor_tensor(out=out.ap, in0=in0.ap, scalar=a, in1=in1.ap,
                                                             op0=op0, op1=op1), rd, [out])

    def copy(self, out, in_, eng="dve"):
        if eng == "act":
            return self.op("act", lambda e: e.copy(out=out.ap, in_=in_.ap), [in_], [out])
        return self.op(eng, lambda e: e.tensor_copy(out=out.ap, in_=in_.ap), [in_], [out])

    def memset(self, out, val, eng="dve"):
        return self.op(eng, lambda e: e.memset(out.ap, val), [], [out])

    def scan(self, out, d0, d1):
        return self.op("dve", lambda e: e.tensor_tensor_scan(out=out.ap, data0=d0.ap, data1=d1.ap, initial=0.0,
                                                             op0=ALU.mult, op1=ALU.add), [d0, d1], [out])

    def recip(self, out, in_):
        return self.op("dve", lambda e: e.reciprocal(out=out.ap, in_=in_.ap), [in_], [out])

    def emit(self):
        nc = self.nc
        ops = self.ops
        eng_ops = {e: [] for e in self.ENGS}
        for i, o in enumerate(ops):
            eng_ops[o["eng"]].append(i)
        sem_of = {}
        dma_sems = {}
        for e in self.ENGS:
            n_comp = sum(1 for i in eng_ops[e] if not ops[i]["dma"])
            for ep in range((n_comp + EPOCH - 1) // EPOCH):
                sem_of[(e, ep)] = self.es.enter_context(nc.semaphore(f"s_{e}_{ep}"))
            if any(ops[i]["dma"] for i in eng_ops[e]):
                dma_sems[e] = [[self.es.enter_context(nc.semaphore(f"d_{e}_{k}")), 0]
                               for k in range(self.n_dma_sems)]
        for e in self.ENGS:
            ci = 0
            di = 0
            for i in eng_ops[e]:
                o = ops[i]
                if o["dma"]:
                    slot = dma_sems[e][di % self.n_dma_sems]
                    di += 1
                    o["prev"] = (slot[0], slot[1])
                    slot[1] += 16
                    o["tick"] = (slot[0], slot[1])
                else:
                    o["tick"] = (sem_of[(e, ci // EPOCH)], ci % EPOCH + 1)
                    ci += 1
        final_waits = []
        for e in self.ENGS:
            if e in dma_sems:
                for s, v in dma_sems[e]:
                    if v > 0:
                        final_waits.append((s, v))
        self.n_inst = {e: len(eng_ops[e]) for e in self.ENGS}
        block = self.es.enter_context(nc.Block())

        def body(e_name):
            def f(eng):
                waited = {}

                def wait(sem, val):
                    k = id(sem)
                    if waited.get(k, 0) >= val:
                        return
                    eng.wait_ge(sem, val)
                    waited[k] = val

                for i in eng_ops[e_name]:
                    o = ops[i]
                    for d in o["deps"]:
                        od = ops[d]
                        if od["eng"] == e_name and not od["dma"] and e_name == "pe":
                            continue
                        s, v = od["tick"]
                        wait(s, v)
                    if o["dma"]:
                        s, v = o["prev"]
                        if v > 0:
                            wait(s, v)
                    ins = o["fn"](eng)
                    s, v = o["tick"]
                    ins.then_inc(s, 16 if o["dma"] else 1)
                if e_name == "sp":
                    for s, v in final_waits:
                        wait(s, v)
            return f

        block.tensor(body("pe"))
        block.scalar(body("act"))
        block.vector(body("dve"))
        block.gpsimd(body("pool"))
        block.sync(body("sp"))

    def close(self):
        self.es.close()


def make_consts():
    c128 = np.zeros((128, 452 + TT), np.float32)
    c128[:, 0:128] = np.eye(128, dtype=np.float32)
    c128[:, 128:256] = 1.0
    c128[0:64, 256:320] = 1.0
    c128[64:128, 320:384] = 1.0
    c128[0:64, 384] = 1.0
    c128[64:128, 385] = 1.0
    c128[0:64, 386] = -1.0
    c128[64:128, 387] = -1.0
    rm = np.ones((TT,), np.float32)
    rm[0::L] = 0.0
    c128[:, 388:388 + TT] = rm[None, :]
    for g, win in enumerate((2, 4, 8, 16)):
        for t in range(16):
            c128[:, 388 + TT + g * 16 + t] = 1.0 / min(t + 1, win)
    i = np.arange(L)[:, None]
    t = np.arange(L)[None, :]
    su = (i < t).astype(np.float32)
    u = (i <= t).astype(np.float32)
    sl = (i > t).astype(np.float32)
    NEG = -30000.0
    parts = [su, u, sl, np.eye(L, dtype=np.float32),
             np.where(t >= i, 0.0, NEG), np.where(t > i, 0.0, NEG), np.where(t < i, 0.0, NEG)]
    c64 = np.zeros((128, 7 * TT + 64), np.float32)
    for k, m in enumerate(parts):
        c64[0:64, k * TT:(k + 1) * TT] = np.tile(m.astype(np.float32), (1, NCH))
    c64[63, 7 * TT:7 * TT + 64] = 1.0
    return c128, c64


IN_TILES = ([(i * 128, 128) for i in range(8)]
            + [(R0 + i * 128, 128) for i in range(24)]
            + [(R0 + 3072, 128), (R0 + 3200, 128), (R0 + 3328, 32)]
            + [(G0 + i * 128, 128) for i in range(32)]
            + [(G0 + 4096, 16)]
            + [(Q0 + i * 128, 128) for i in range(48)])
T_POOL, T_R, T_K, T_V, T_LO, T_GQ, T_GK, T_GV, T_GZ, T_GAB, T_GATE = 0, 8, 16, 24, 32, 35, 43, 51, 59, 67, 68
UP_TILES = []
for _i in range(KTF):
    UP_TILES += [(_i * 128, 128), (DFF + _i * 128, 128)]
D_TILES = [(i * 128, 128) for i in range(16)]

O_MOD, O_LN1W, O_LN1B, O_LN2W, O_LN2B, O_PSC, O_MU = 0, 96, 112, 128, 144, 160, 168
O_W0, O_A0, O_KK, O_KA, O_KA1, O_RK, O_LNW, O_LNB, O_V0 = 195, 203, 211, 219, 227, 235, 243, 251, 259
O_CONV, O_GNW, NPAR = 267, 363, 364


def build(T, NL, dbg=None, NL_main=None):
    NTILES = T // TT
    nc = bass.Bass("TRN2", target_bir_lowering=False)
    P = Prog(nc)

    def din(name, shape):
        return nc.dram_tensor(name, list(shape), F32, kind="ExternalInput").ap()

    x_d = din("x", [T, D])
    c_d = din("c", [D])
    ada_w_d = din("ada_w", [2, D, 6 * D])
    ada_b_d = din("ada_b", [2, 6 * D])
    w_in_d = din("w_in", [2, D, N_IN])
    pool_w_d = din("pool_w", [2, 4, 256, 256])
    pool_scale_d = din("pool_scale", [2, C])
    mu_d = din("rwkv_mu", [2, 3360])
    w0_d = din("rwkv_w0", [2, C])
    w2_d = din("rwkv_w2", [2, 64, C])
    a0_d = din("rwkv_a0", [2, C])
    a2_d = din("rwkv_a2", [2, 64, C])
    g2_d = din("rwkv_g2", [2, 160, C])
    kk_d = din("rwkv_k_k", [2, C])
    ka_d = din("rwkv_k_a", [2, C])
    rk_d = din("rwkv_r_k", [2, C])
    lnw_d = din("rwkv_ln_w", [2, C])
    lnb_d = din("rwkv_ln_b", [2, C])
    v0_d = din("rwkv_v0", [1, C])
    v1_d = din("rwkv_v1", [1, C, 32])
    v2_d = din("rwkv_v2", [1, 32, C])
    conv_d = din("gdn_conv_w", [2, 4, 3 * C])
    alog_d = din("gdn_a_log", [2, 8])
    dtb_d = din("gdn_dt_bias", [2, 8])
    gnw_d = din("gdn_norm_w", [2, 128])
    wbr_d = [din("w_branch_a", [2, C, D]), din("w_branch_b", [2, C, D]), din("w_branch_c", [2, C, D])]
    wout_d = din("w_out", [2, D, D])
    ln1w_d = din("ln1_w", [2, D])
    ln1b_d = din("ln1_b", [2, D])
    up_d = din("ffn_w_up", [2, D, 2 * DFF])
    dn_d = din("ffn_w_down", [2, DFF, D])
    ln2w_d = din("ln2_w", [2, D])
    ln2b_d = din("ln2_b", [2, D])
    c128_d = din("c128", [128, 452 + TT])
    c64_d = din("c64", [128, 7 * TT + 64])
    out_d = nc.dram_tensor("out", [T, D], F32, kind="ExternalOutput").ap()
    mod_s = nc.dram_tensor("mod_s", [2, 6 * D], F32, kind="Internal").ap()
    dbg_d = nc.dram_tensor("dbg", [3, 128, 8, TT], BF16, kind="ExternalOutput").ap() if dbg is not None else None
    dbgh_d = nc.dram_tensor("dbgh", [128, KT, TT], BF16, kind="ExternalOutput").ap() if dbg is not None else None
    dbgc_d = nc.dram_tensor("dbgc", [128, NPAR], F32, kind="ExternalOutput").ap() if dbg is not None else None
    dbgx_d = nc.dram_tensor("dbgx", [12, 128, TT], F32, kind="ExternalOutput").ap() if dbg is not None else None
    mod_trk = [Trk("mod_s")]
    ada_trk = [Trk("ada_w")]

    xT = P.sb("xT", [128, KT, TT], F32, split=True)
    hT = P.sb("hT", [128, KT, TT], BF16, split=True)
    vfirst = P.sb("vfirst", [128, 8, TT], F32, split=True)
    yb = [P.sb(f"y{b}", [128, 8, TT], BF16, split=True) for b in range(3)]
    merged = P.sb("merged", [128, KT, TT], BF16, split=True)
    NWS = 3
    WSLOT = 6144
    wbuf = [P.sb(f"wbuf{i}", [128, WSLOT], BF16) for i in range(NWS)]
    cols = [P.sb(f"cols{l}", [128, NPAR], F32) for l in range(2)]
    g16 = [P.sb(f"g16_{l}", [16, 2], F32) for l in range(2)]
    c128 = P.sb("c128", [128, 452 + TT], F32)
    c64 = P.sb("c64", [128, 7 * TT + 64], F32)
    condT = P.sb("condT", [128, 16], F32)
    vstage = P.sb("vstage", [128, 128], F32)
    rwH = [P.sb(f"rwH{l}", [128, 8, 64], F32, split=True) for l in range(2)]
    gdS = [P.sb(f"gdS{l}", [128, 8, 128], F32, split=True) for l in range(2)]
    pooltail = [P.sb(f"ptail{l}", [128, 8, 15], F32) for l in range(2)]
    shtail = [P.sb(f"shtail{l}", [128, 27], F32) for l in range(2)]
    cvtail = [P.sb(f"cvtail{l}", [128, 24, 3], F32) for l in range(2)]
    SMALL = P.sb("SMALL", [128, 6144], BF16)
    small_s = nc.dram_tensor("small_s", [2, 128, 6144], BF16, kind="Internal").ap()
    small_trk = [[Trk("small_s0")], [Trk("small_s1")]]

    class _SV:
        def __init__(self, a, b_, rows=128):
            self.t = SMALL.t[0:rows, a:b_]
            self.trs = SMALL.trs

        def all(self):
            return V(self.t, self.trs)
    W2z = [_SV(0, 1024)] * 2
    A2z = [_SV(1024, 2048)] * 2
    G2a = [_SV(2048, 3072)] * 2
    G2b = [_SV(3072, 4096, 128)] * 2
    V1 = P.sb("V1", [128, 8, 32], BF16)
    V2 = P.sb("V2", [128, C], BF16)
    class _PW:
        t = SMALL.t[:, 4096:6144].rearrange("p (g k d) -> p g k d", g=4, k=2)
        trs = SMALL.trs

        def all(self):
            return V(self.t, self.trs)
    POOLW = [_PW()] * 2
    ARENA_N = 16896
    arena = P.sb("arena", [128, ARENA_N], F32)

    banks = [P.ps(f"bank{i}", [128, 512]) for i in range(8)]
    rr = [0]

    def nb():
        b = banks[rr[0] % 5]
        rr[0] += 1
        return b

    BANK_Y, BANK_H = banks[5], banks[6]

    ident = V(c128.t[:, 0:128], c128.trs)
    ones = V(c128.t[:, 128:256], c128.trs)
    blk = V(c128.t[:, 256:384], c128.trs)

    def mcol(i):
        return V(c128.t[:, 384 + i:385 + i], c128.trs)

    rmask = V(c128.t[:, 388:388 + TT], c128.trs)

    def c64v(k):
        return V(c64.t[0:64, k * TT:(k + 1) * TT], c64.trs)

    M_SU, M_U, M_SL, M_ID, GB_U, GB_SU, GB_SL = [c64v(k) for k in range(7)]
    E63 = V(c64.t[:, 7 * TT:7 * TT + 64], c64.trs)

    class Arena:
        def __init__(self):
            self.off = 0
            self.views = []
            self.pads = []

        def reset(self):
            self.off = 0

        def f32(self, name, parts, n):
            v = V(arena.t[0:parts, self.off:self.off + n], [Trk(name)])
            self.off += n
            assert self.off <= ARENA_N, (name, self.off)
            self.views.append(v)
            return v

        def bf16(self, name, parts, n):
            n32 = (n + 1) // 2
            ap = arena.t[0:parts, self.off:self.off + n32].bitcast(BF16)
            v = V(ap, [Trk(name)])
            self.off += n32
            assert self.off <= ARENA_N, (name, self.off)
            self.views.append(v)
            return v

    AR = Arena()
    FULL = {}

    def pad(name, rows, n, bf=False):
        if bf:
            n32 = (n + 1) // 2
            base = arena.t[:, AR.off:AR.off + n32].bitcast(BF16)
        else:
            n32 = n
            base = arena.t[:, AR.off:AR.off + n32]
        AR.off += n32
        assert AR.off <= ARENA_N, (name, AR.off)
        tr = [Trk(name)]
        lo = V(base[0:rows, :], tr)
        fu = V(base, tr)
        FULL[id(tr[0])] = fu
        AR.views.append(fu)
        AR.pads.append(fu)
        return lo

    def Fv(v):
        return FULL[id(v.tr[0])]

    def zero_pads():
        for i, fu in enumerate(AR.pads):
            P.memset(fu, 0.0, eng=("pool" if i % 2 else "dve"))
        AR.pads = []
    fence_scr = P.sb("fence", [128, 1], F32)

    def fence(old_views, new_views):
        P.op("dve", lambda e: e.memset(fence_scr.t[:], 0.0), [], list(old_views) + list(new_views) + [fence_scr.all()])

    cyc = [0]

    def eng2():
        cyc[0] += 1
        return "dve" if cyc[0] % 2 else "pool"

    P.dma("sp", c128.all(), c128_d)
    P.dma("sp", c64.all(), c64_d)
    for l in range(2):
        P.memset(rwH[l].all(), 0.0)
        P.memset(gdS[l].all(), 0.0, eng="pool")
        P.memset(pooltail[l].all(), 0.0)
        P.memset(shtail[l].all(), 0.0, eng="pool")
        P.memset(cvtail[l].all(), 0.0)

    def load_vec_cols(src1d, n, dst, nlast=128, src_tr=None):
        P.memset(vstage.all(), 0.0)
        if src_tr is not None:
            P.dma("sp", vstage.all()[0:n, :], V(src1d.rearrange("(j p) -> j p", p=128), src_tr))
        elif nlast != 128:
            P.dma("sp", vstage.all()[0:n - 1, :], src1d[0:(n - 1) * 128].rearrange("(j p) -> j p", p=128))
            P.dma("sp", vstage.all()[n - 1:n, 0:nlast],
                  src1d[(n - 1) * 128:(n - 1) * 128 + nlast].rearrange("(j p) -> j p", p=nlast))
        else:
            P.dma("sp", vstage.all()[0:n, :], src1d.rearrange("(j p) -> j p", p=128))
        b = nb()
        P.mm(b.all()[:, 0:n], vstage.all(), ident[:, 0:n])
        P.copy(dst, b.all()[:, 0:n], eng="act")

    def colv(l, off, n=1):
        return V(cols[l].t[:, off:off + n], cols[l].trs)

    load_vec_cols(c_d, 16, condT.all())
    P.act(condT.all(), condT.all(), AF.Silu)

    AR.reset()
    stg = [AR.f32(f"stg{i}", 128, 5632) for i in range(2)]
    stb = [AR.bf16(f"stb{i}", 128, 5632) for i in range(2)]
    modrow = [P.sb(f"modrow{i}", [1, 256], F32).all() for i in range(2)]
    modtmp = P.sb("modtmp", [128, 96], F32).all()
    prologue_views = list(AR.views)
    sidx = [0]

    for l in range(NL):
        for cb in range(48):
            st = stg[sidx[0] % 2]
            sidx[0] += 1
            stv = V(st.ap[:, 0:KT * 256].rearrange("p (k c) -> p k c", k=KT), st.tr)
            P.dma("sp", stv, V(ada_w_d[l][:, cb * 256:(cb + 1) * 256].rearrange("(k p) c -> p k c", p=128), ada_trk))
            b = nb()
            for kt in range(KT):
                P.mm(b.all()[0:1, 0:256], condT.all()[:, kt:kt + 1], stv[:, kt, :], start=(kt == 0), stop=(kt == KT - 1))
            mr = modrow[cb % 2]
            P.copy(mr, b.all()[0:1, 0:256], eng="act")
            P.dma("pool", V(mod_s[l:l + 1, cb * 256:(cb + 1) * 256], mod_trk), mr)

    P.op("dve", lambda e: e.memset(fence_scr.t[:], 0.0), [], [V(None, ada_trk), fence_scr.all()])

    def vec_params(l):
        load_vec_cols(mod_s[l], 96, colv(l, O_MOD, 96), src_tr=mod_trk)
        load_vec_cols(ada_b_d[l], 96, modtmp)
        P.tt(colv(l, O_MOD, 96), colv(l, O_MOD, 96), modtmp, ALU.add)
        for o in (16, 32, 64, 80):
            P.ts(colv(l, O_MOD + o, 16), colv(l, O_MOD + o, 16), 1.0, ALU.add)
        load_vec_cols(ln1w_d[l], 16, colv(l, O_LN1W, 16))
        load_vec_cols(ln1b_d[l], 16, colv(l, O_LN1B, 16))
        load_vec_cols(ln2w_d[l], 16, colv(l, O_LN2W, 16))
        load_vec_cols(ln2b_d[l], 16, colv(l, O_LN2B, 16))
        load_vec_cols(pool_scale_d[l], 8, colv(l, O_PSC, 8))
        load_vec_cols(mu_d[l], 27, colv(l, O_MU, 27), nlast=32)
        load_vec_cols(w0_d[l], 8, colv(l, O_W0, 8))
        load_vec_cols(a0_d[l], 8, colv(l, O_A0, 8))
        load_vec_cols(kk_d[l], 8, colv(l, O_KK, 8))
        load_vec_cols(ka_d[l], 8, colv(l, O_KA, 8))
        P.ts(colv(l, O_KA1, 8), colv(l, O_KA, 8), -1.0, ALU.mult, 1.0, ALU.add)
        load_vec_cols(rk_d[l], 8, colv(l, O_RK, 8))
        load_vec_cols(lnw_d[l], 8, colv(l, O_LNW, 8))
        load_vec_cols(lnb_d[l], 8, colv(l, O_LNB, 8))
        if l >= 1:
            load_vec_cols(v0_d[l - 1], 8, colv(l, O_V0, 8))
        for j in range(4):
            load_vec_cols(conv_d[l, j], 24, colv(l, O_CONV + 24 * j, 24))
        load_vec_cols(gnw_d[l], 1, colv(l, O_GNW, 1))
        P.memset(g16[l].all(), 0.0)
        P.dma("sp", g16[l].all()[0:8, 0:1], dtb_d[l].rearrange("(p o) -> p o", o=1))
        P.dma("sp", g16[l].all()[0:8, 1:2], alog_d[l].rearrange("(p o) -> p o", o=1))
        P.act(g16[l].all()[:, 1:2], g16[l].all()[:, 1:2], AF.Exp)
        P.ts(g16[l].all()[:, 1:2], g16[l].all()[:, 1:2], -1.0, ALU.mult)

    ada_flat = ada_w_d.rearrange("l k n -> (l k n)")
    carve_off = [0]

    def conv_weight(name, src2d, K, tiles, carve=False):
        KTn = K // 128
        if carve:
            n32 = len(tiles) * 128 * KTn * 128 // 2
            scr = ada_flat[carve_off[0]:carve_off[0] + n32].bitcast(BF16).rearrange(
                "(t p k c) -> t p k c", t=len(tiles), p=128, k=KTn)
            carve_off[0] += n32
            assert carve_off[0] <= 2 * D * 6 * D
        else:
            scr = nc.dram_tensor(name, [len(tiles), 128, KTn, 128], BF16, kind="Internal").ap()
        scr_v = V(scr, [Trk(f"{name}_{t}") for t in range(len(tiles))])
        maxc = min(512, (5632 // KTn) // 128 * 128)
        i = 0
        while i < len(tiles):
            j = i
            ncols = tiles[i][1]
            while (j + 1 < len(tiles) and tiles[j][1] == 128 and tiles[j + 1][0] == tiles[j][0] + 128
                   and ncols + tiles[j + 1][1] <= maxc):
                j += 1
                ncols += tiles[j][1]
            c0 = tiles[i][0]
            k = sidx[0] % 2
            sidx[0] += 1
            sv = V(stg[k].ap[:, 0:KTn * ncols].rearrange("p (k c) -> p k c", k=KTn), stg[k].tr)
            bv = V(stb[k].ap[:, 0:KTn * ncols].rearrange("p (k c) -> p k c", k=KTn), stb[k].tr)
            P.dma("sp", sv, src2d[:, c0:c0 + ncols].rearrange("(k p) c -> p k c", p=128))
            e = ("dve", "pool", "act")[sidx[0] % 3]
            P.copy(bv, sv, eng=e)
            off = 0
            for t in range(i, j + 1):
                w = tiles[t][1]
                P.dma("pool", V(scr[t][:, :, 0:w], [scr_v.tr[t]]), bv[:, :, off:off + w],
                      reads=([V(None, ada_trk)] if carve else []))
                off += w
            i = j + 1
        return scr_v

    WIN, WUP, WDN, WOUT, WBR = [], [], [], [], []
    for l in range(NL):
        WIN.append(conv_weight(f"win{l}", w_in_d[l], D, IN_TILES, carve=(l > 0)))
        WBR.append([conv_weight(f"wbr{l}_{b}", wbr_d[b][l], C, D_TILES, carve=(l > 0)) for b in range(3)])
        WOUT.append(conv_weight(f"wout{l}", wout_d[l], D, D_TILES, carve=(l > 0)))
        WUP.append(conv_weight(f"wup{l}", up_d[l], D, UP_TILES, carve=(l > 0)))
        WDN.append(conv_weight(f"wdn{l}", dn_d[l], DFF, D_TILES, carve=(l > 0)))

    def small_w(dst, src2d, rows, ncols, row0=0, zero=False):
        k = sidx[0] % 2
        sidx[0] += 1
        sv = V(stg[k].ap[:, 0:ncols], stg[k].tr)
        if zero:
            P.memset(sv, 0.0)
        P.dma("sp", sv[row0:row0 + rows, :], src2d)
        if zero:
            P.copy(dst, sv, eng="dve")
        else:
            P.copy(dst, sv[row0:row0 + rows, :], eng="dve")

    for l in range(NL):
        small_w(W2z[l].all(), w2_d[l], 64, C, 0, zero=True)
        small_w(A2z[l].all(), a2_d[l], 64, C, 64, zero=True)
        small_w(G2a[l].all(), g2_d[l][0:128], 128, C)
        P.memset(G2b[l].all(), 0.0)
        small_w(V(SMALL.t[0:32, 3072:4096], SMALL.trs), g2_d[l][128:160], 32, C)
        k = sidx[0] % 2
        sidx[0] += 1
        sv = V(stg[k].ap[:, 0:2048].rearrange("p (g k d) -> p g k d", g=4, k=2), stg[k].tr)
        for g in range(4):
            P.dma("sp", sv[:, g], pool_w_d[l, g].rearrange("(k p) d -> p k d", p=128))
        P.copy(POOLW[l].all(), sv, eng="dve")
        P.dma("pool", V(small_s[l], small_trk[l]), SMALL.all())
    if NL > 1:
        k = sidx[0] % 2
        sidx[0] += 1
        sv = V(stg[k].ap[:, 0:256].rearrange("p (k c) -> p k c", k=8), stg[k].tr)
        P.dma("sp", sv, v1_d[0].rearrange("(k p) c -> p k c", p=128))
        P.copy(V1.all(), sv, eng="dve")
        P.memset(V2.all(), 0.0)
        small_w(V(V2.t[0:32, :], V2.trs), v2_d[0], 32, C)

    for l in range(NL):
        vec_params(l)

    def layout_main():
        AR.reset()
        AR.views = []
        d = {}
        d["XS"] = AR.f32("XS", 128, 2 * D)
        return d

    def w_load(scr_v, t0, n, KTn):
        slot = wbuf[w_load.i % NWS]
        w_load.i += 1
        sv = V(slot.t[:, 0:n * KTn * 128].rearrange("p (t k c) -> p t k c", t=n, k=KTn), slot.trs)
        P.dma("sp", sv, V(scr_v.ap[t0:t0 + n].rearrange("t p k c -> p t k c"), scr_v.tr[t0:t0 + n]))
        return sv
    w_load.i = 0

    def proj(scr_v, KTn, t0, n, rhs, evac, widths=None, per_load=None):
        if per_load is None:
            per_load = max(1, WSLOT // (KTn * 128))
        i = 0
        while i < n:
            m = min(per_load, n - i)
            wv = w_load(scr_v, t0 + i, m, KTn)
            for q in range(m):
                w = 128 if widths is None else widths[i + q]
                b = nb()
                pv = b.all()[0:w, 0:TT]
                for kt in range(KTn):
                    P.mm(pv, wv[:, q, kt, 0:w], rhs(kt), start=(kt == 0), stop=(kt == KTn - 1))
                evac(i + q, pv, w)
            i += m

    def layer_norm(l, o_w, o_b, st1, st2, st3):
        b1 = nb()
        for kt in range(KT):
            P.mm(b1.all()[:, 0:TT], ones, xT[kt], start=(kt == 0), stop=(kt == KT - 1))
        b2 = nb()
        for kt in range(KT):
            sq = st3[kt % 2]
            P.act(sq, xT[kt], AF.Square)
            P.mm(b2.all()[:, 0:TT], ones, sq, start=(kt == 0), stop=(kt == KT - 1))
        mean, rstd = st1, st2
        P.act(mean, b1.all()[:, 0:TT], AF.Copy, scale=1.0 / D)
        P.act(rstd, b2.all()[:, 0:TT], AF.Copy, scale=1.0 / D)
        msq = st3[0]
        P.tt(msq, mean, mean, ALU.mult)
        P.tt(rstd, rstd, msq, ALU.subtract)
        P.act(rstd, rstd, AF.Sqrt, bias=LN_EPS)
        P.recip(rstd, rstd)
        for kt in range(KT):
            e = eng2()
            P.tt(xT[kt], xT[kt], mean, ALU.subtract, eng=e)
            P.tt(xT[kt], xT[kt], rstd, ALU.mult, eng=e)
            P.ts(xT[kt], xT[kt], colv(l, o_w + kt), ALU.mult, colv(l, o_b + kt), ALU.add, eng=e)

    def inv_chain(N0, A0, TTm, tmpN, tmpA):
        def blkv(v, c):
            return v[:, c * L:(c + 1) * L]

        def blkf(v, c):
            return Fv(v)[:, c * L:(c + 1) * L]
        P.tt(TTm, M_ID, N0, ALU.add)
        Nk, Ak = N0, A0
        for lev in range(1, 6):
            An = tmpA[lev % 2]
            b = nb()
            for c in range(NCH):
                P.mm(blkv(b.all()[0:64, 0:TT], c), blkf(Nk, c), blkf(Ak, c))
            P.copy(An, b.all()[0:64, 0:TT], eng="act")
            if lev < 5:
                Nn = tmpN[lev % 2]
                b = nb()
                for c in range(NCH):
                    P.mm(blkv(b.all()[0:64, 0:TT], c), blkf(Ak, c), blkf(Nk, c))
                P.copy(Nn, b.all()[0:64, 0:TT], eng="act")
            b = nb()
            for c in range(NCH):
                P.mm(blkv(b.all()[0:64, 0:TT], c), blkf(An, c), blkf(TTm, c))
            P.tt(TTm, b.all()[0:64, 0:TT], TTm, ALU.add)
            Ak = An
            if lev < 5:
                Nk = Nn

    prev_views = list(prologue_views)

    for ti in range(NTILES):
        tok0 = ti * TT
        AR.reset()
        AR.views = []
        XS = AR.f32("XS", 128, 2 * D)
        fence(prev_views, AR.views)
        prev_views = list(AR.views)
        XSv = V(XS.ap.rearrange("p (b d) -> p b d", b=2), XS.tr)
        P.dma("sp", XSv, x_d[tok0:tok0 + TT, :].rearrange("(b p) d -> p b d", p=128))
        for kt in range(KT):
            b = nb()
            for tb in range(2):
                P.mm(b.all()[:, tb * 128:(tb + 1) * 128], XSv[:, tb, kt * 128:(kt + 1) * 128], ident)
            P.copy(xT[kt], b.all()[:, 0:TT], eng="act")

        for l in range(NL if NL_main is None else NL_main):
            if NL > 1 or ti == 0:
                P.dma("sp", SMALL.all(), V(small_s[l], small_trk[l]))
            for kt in range(KT):
                P.ts(hT[kt], xT[kt], colv(l, O_MOD + 16 + kt), ALU.mult, colv(l, O_MOD + kt), ALU.add, eng=eng2())

            AR.reset()
            AR.views = []
            PP = AR.f32("PP", 128, 8 * (15 + TT))
            WA = AR.f32("WA", 128, 8 * (15 + TT))
            WB = AR.f32("WB", 128, 8 * (15 + TT))
            PLD = AR.bf16("PLD", 128, 8 * TT)
            PT16 = AR.f32("PT16", 128, 2 * 16)
            fence(prev_views, AR.views)
            prev_views = list(AR.views)
            W_ = 15 + TT

            def v3(v):
                return V(v.ap.rearrange("p (g w) -> p g w", g=8), v.tr)
            PP3, WA3, WB3 = v3(PP), v3(WA), v3(WB)
            PLD3 = V(PLD.ap.rearrange("p (g w) -> p g w", g=8), PLD.tr)
            P.copy(PP3[:, :, 0:15], pooltail[l].all())

            def ev_pool(i, pv, w):
                P.copy(PP3[:, i, 15:15 + TT], pv, eng="act")
            proj(WIN[l], KT, T_POOL, 8, lambda kt: hT[kt], ev_pool)
            P.copy(pooltail[l].all(), PP3[:, :, TT:TT + 15], eng="pool")
            P.tt(WA3[:, :, 1:W_], PP3[:, :, 1:W_], PP3[:, :, 0:W_ - 1], ALU.add)
            P.tt(WB3[:, 2:8, 3:W_], WA3[:, 2:8, 3:W_], WA3[:, 2:8, 1:W_ - 2], ALU.add, eng="pool")
            P.tt(WA3[:, 4:8, 7:W_], WB3[:, 4:8, 7:W_], WB3[:, 4:8, 3:W_ - 4], ALU.add)
            P.tt(WB3[:, 6:8, 15:W_], WA3[:, 6:8, 15:W_], WA3[:, 6:8, 7:W_ - 8], ALU.add, eng="pool")
            srcs = [WA3, WB3, WA3, WB3]
            for g, win in enumerate((2, 4, 8, 16)):
                sl_ = slice(2 * g, 2 * g + 2)
                P.stt(PLD3[:, sl_, :], srcs[g][:, sl_, 15:W_], 1.0 / win, PP3[:, sl_, 15:W_], ALU.mult, ALU.subtract)
                if ti == 0:
                    icv = V(c128.t[:, 388 + TT + g * 16:388 + TT + (g + 1) * 16].unsqueeze(1).to_broadcast([128, 2, 16]),
                            c128.trs)
                    t16 = V(PT16.ap.rearrange("p (g w) -> p g w", g=2), PT16.tr)
                    P.tt(t16, srcs[g][:, sl_, 15:31], icv, ALU.mult)
                    P.tt(PLD3[:, sl_, 0:16], t16, PP3[:, sl_, 15:31], ALU.subtract)
            for g in range(4):
                for dt in range(2):
                    b = nb()
                    for ct in range(2):
                        P.mm(b.all()[:, 0:TT], V(POOLW[l].t[:, g, ct, dt * 128:(dt + 1) * 128], POOLW[l].trs),
                             PLD3[:, 2 * g + ct, :], start=(ct == 0), stop=(ct == 1))
                    P.ts(yb[0][2 * g + dt], b.all()[:, 0:TT], colv(l, O_PSC + 2 * g + dt), ALU.mult)

            AR.reset()
            AR.views = []
            f = lambda n, p=128, w=TT: AR.f32(n, p, w)
            PS_ = AR.f32("PS", 128, 1 + TT)
            DIF = f("DIF")
            XSL = f("XSL")
            L1 = AR.bf16("L1", 128, TT)
            L2 = AR.bf16("L2", 128, TT)
            L3 = pad("L3", 32, TT, bf=True)
            VX = AR.f32("VX", 128, 8 * TT) if l > 0 else None
            VXB = AR.bf16("VXB", 128, 8 * TT) if l > 0 else None
            VVS = pad("VVS", 32, TT, bf=True) if l > 0 else None
            R_, K0, SIG, AA, GG, KKt, KM, BON, CUM, T1, T2, T3 = [f(n) for n in
                ("R", "K0", "SIG", "AA", "GG", "KK", "KM", "BON", "CUM", "T1", "T2", "T3")]
            AZ = [f("AZ0"), f("AZ1")]
            BZ = [f("BZ0"), f("BZ1")]
            KZ = [f("KZ0"), f("KZ1")]
            RZ = [f("RZ0"), f("RZ1")]
            BTM = pad("BTM", 64, NCH * 128)
            KTM = pad("KTM", 64, NCH * 128)
            VTM = pad("VTM", 64, NCH * 128)
            g64 = lambda n: pad(n, 64, TT)
            AAK = [g64("AAK0"), g64("AAK1")]
            ARB = [g64("ARB0"), g64("ARB1")]
            ARK = [g64("ARK0"), g64("ARK1")]
            TTM = [g64("TTM0"), g64("TTM1")]
            N0, A0 = g64("N0"), g64("A0")
            tmpN = [g64("tN0"), g64("tN1")]
            tmpA = [g64("tA0"), g64("tA1")]
            W0S = pad("W0S", 64, 64)
            US = [pad("US0", 64, 64), pad("US1", 64, 64)]
            HT_ = AR.f32("HT", 128, 64)
            fence(prev_views, AR.views)
            prev_views = list(AR.views)
            zero_pads()

            def shift_xs(pv, w, tix, dst):
                P.copy(PS_[0:w, 1:1 + TT], pv, eng="act")
                P.copy(PS_[0:w, 0:1], V(shtail[l].t[0:w, tix:tix + 1], shtail[l].trs), eng="pool")
                P.tt(DIF[0:w, :], PS_[0:w, 0:TT], PS_[0:w, 1:1 + TT], ALU.subtract)
                P.copy(V(shtail[l].t[0:w, tix:tix + 1], shtail[l].trs), PS_[0:w, TT:TT + 1], eng="pool")
                P.stt(dst, DIF[0:w, :], V(cols[l].t[0:w, O_MU + tix:O_MU + tix + 1], cols[l].trs),
                      PS_[0:w, 1:1 + TT], ALU.mult, ALU.add)

            def ev_lora(i, pv, w):
                shift_xs(pv, w, 24 + i, XSL[0:w, :])
                if i == 0:
                    P.act(L1[0:64, :], XSL[0:64, :], AF.Tanh)
                    P.copy(L1[64:128, :], XSL[64:128, :], eng="pool")
                elif i == 1:
                    P.act(L2, XSL, AF.Sigmoid)
                else:
                    P.act(L3, XSL[0:32, :], AF.Sigmoid)
            proj(WIN[l], KT, T_LO, 3, lambda kt: hT[kt], ev_lora, widths=[128, 128, 32])

            if l > 0:
                VX3 = V(VX.ap.rearrange("p (j w) -> p j w", j=8), VX.tr)
                VXB3 = V(VXB.ap.rearrange("p (j w) -> p j w", j=8), VXB.tr)

                def ev_v(i, pv, w):
                    shift_xs(pv, w, 16 + i, VX3[:, i, :])
                    P.copy(VXB3[:, i, :], VX3[:, i, :], eng="pool")
                proj(WIN[l], KT, T_V, 8, lambda kt: hT[kt], ev_v)
                b = nb()
                for kt in range(8):
                    P.mm(b.all()[0:32, 0:TT], V(V1.t[:, kt, :], V1.trs), VXB3[:, kt, :], start=(kt == 0), stop=(kt == 7))
                P.copy(VVS, b.all()[0:32, 0:TT], eng="act")

            for j in range(8):
                jc = slice(j * 128, (j + 1) * 128)
                Vt = vfirst[j] if l == 0 else VX3[:, j, :]
                proj(WIN[l], KT, T_R + j, 1, lambda kt: hT[kt], lambda i, pv, w: shift_xs(pv, w, j, R_))
                proj(WIN[l], KT, T_K + j, 1, lambda kt: hT[kt], lambda i, pv, w: shift_xs(pv, w, 8 + j, K0))
                if l == 0:
                    proj(WIN[l], KT, T_V + j, 1, lambda kt: hT[kt], lambda i, pv, w: shift_xs(pv, w, 16 + j, Vt))
                b = nb()
                P.mm(b.all()[:, 0:TT], V(W2z[l].t[:, jc], W2z[l].trs), L1)
                P.act(SIG, b.all()[:, 0:TT], AF.Sigmoid, bias=colv(l, O_W0 + j))
                b = nb()
                P.mm(b.all()[:, 0:TT], V(A2z[l].t[:, jc], A2z[l].trs), L1)
                P.act(AA, b.all()[:, 0:TT], AF.Sigmoid, bias=colv(l, O_A0 + j))
                b = nb()
                P.mm(b.all()[:, 0:TT], V(G2a[l].t[:, jc], G2a[l].trs), L2, start=True, stop=False)
                P.mm(b.all()[:, 0:TT], V(G2b[l].t[:, jc], G2b[l].trs), Fv(L3), start=False, stop=True)
                P.copy(GG, b.all()[:, 0:TT], eng="act")
                if l > 0:
                    b = nb()
                    P.mm(b.all()[:, 0:TT], V(V2.t[:, jc], V2.trs), Fv(VVS))
                    P.act(T1, b.all()[:, 0:TT], AF.Sigmoid, bias=colv(l, O_V0 + j))
                    P.tt(T2, vfirst[j], Vt, ALU.subtract)
                    P.tt(T2, T2, T1, ALU.mult)
                    P.tt(Vt, Vt, T2, ALU.add)
                P.ts(KKt, K0, colv(l, O_KK + j), ALU.mult)
                P.tt(T1, KKt, KKt, ALU.mult, eng="pool")
                b = nb()
                P.mm(b.all()[:, 0:TT], blk, T1)
                P.act(T1, b.all()[:, 0:TT], AF.Sqrt)
                P.ts(T1, T1, 1e-12, ALU.max)
                P.recip(T1, T1)
                P.tt(KKt, KKt, T1, ALU.mult)
                P.ts(T2, AA, colv(l, O_KA + j), ALU.mult, colv(l, O_KA1 + j), ALU.add, eng="pool")
                P.tt(KM, K0, T2, ALU.mult, eng="pool")
                P.stt(T2, R_, colv(l, O_RK + j), KM, ALU.mult, ALU.mult)
                b = nb()
                P.mm(b.all()[:, 0:TT], blk, T2)
                P.tt(BON, b.all()[:, 0:TT], Vt, ALU.mult)
                P.scan(CUM, rmask, SIG)
                if dbg is not None and dbg == (ti, l) and j == 0:
                    P.dma("pool", dbgx_d[9], CUM)
                P.tt(T1, CUM, SIG, ALU.subtract, eng="pool")
                P.act(T1, T1, AF.Exp, scale=-C0)
                P.act(T2, CUM, AF.Exp, scale=-C0)
                P.act(T3, CUM, AF.Exp, scale=C0)
                P.tt(K0, KKt, AA, ALU.mult, eng="pool")
                for h in range(2):
                    P.stt(AZ[h], KKt, mcol(2 + h), T1, ALU.mult, ALU.mult, eng=eng2())
                    P.stt(BZ[h], K0, mcol(h), T3, ALU.mult, ALU.mult, eng=eng2())
                    P.stt(KZ[h], KM, mcol(h), T3, ALU.mult, ALU.mult, eng=eng2())
                    P.stt(RZ[h], R_, mcol(h), T2, ALU.mult, ALU.mult, eng=eng2())
                for (dst, srcs_) in ((BTM, BZ), (KTM, KZ), (VTM, [Vt])):
                    b = nb()
                    for c in range(NCH):
                        for si, s in enumerate(srcs_):
                            P.mm(b.all()[0:64, c * 128:(c + 1) * 128], s[:, c * L:(c + 1) * L], ident,
                                 start=(si == 0), stop=(si == len(srcs_) - 1))
                    P.copy(dst, b.all()[0:64, 0:NCH * 128], eng="act")
                for h in range(2):
                    def amat(dst, lh, rh, mask):
                        b = nb()
                        for c in range(NCH):
                            cs = slice(c * L, (c + 1) * L)
                            P.mm(b.all()[0:64, cs], lh[:, cs], rh[:, cs])
                        P.tt(dst, b.all()[0:64, 0:TT], mask, ALU.mult)
                    amat(N0, BZ[h], AZ[h], M_SU)
                    amat(A0, AZ[h], BZ[h], M_SL)
                    amat(AAK[h], KZ[h], AZ[h], M_SU)
                    amat(ARB[h], BZ[h], RZ[h], M_U)
                    amat(ARK[h], KZ[h], RZ[h], M_U)
                    inv_chain(N0, A0, TTM[h], tmpN, tmpA)
                Hp = rwH[l][j]
                for c in range(NCH):
                    cs = slice(c * L, (c + 1) * L)
                    for h in range(2):
                        hs = slice(c * 128 + h * 64, c * 128 + (h + 1) * 64)
                        hp = slice(h * 64, (h + 1) * 64)
                        b = nb()
                        P.mm(b.all()[0:64, 0:64], AZ[h][:, cs], Hp, start=True, stop=False)
                        P.mm(b.all()[0:64, 0:64], Fv(AAK[h])[:, cs], Fv(VTM)[:, hs], start=False, stop=True)
                        P.copy(W0S, b.all()[0:64, 0:64], eng="act")
                        b = nb()
                        P.mm(b.all()[0:64, 0:64], Fv(TTM[h])[:, cs], Fv(W0S))
                        P.copy(US[h], b.all()[0:64, 0:64], eng="act")
                        yv = BANK_Y.all()[hp, cs]
                        P.mm(yv, Hp, RZ[h][:, cs], start=True, stop=False)
                        P.mm(yv, Fv(US[h]), Fv(ARB[h])[:, cs], start=False, stop=False)
                        P.mm(yv, Fv(VTM)[:, hs], Fv(ARK[h])[:, cs], start=False, stop=True)
                        hv = BANK_H.all()[hp, 0:64]
                        P.mm(hv, Fv(BTM)[:, hs], Fv(US[h]), start=True, stop=False)
                        P.mm(hv, Fv(KTM)[:, hs], Fv(VTM)[:, hs], start=False, stop=True)
                    P.tt(HT_, BANK_H.all()[:, 0:64], Hp, ALU.add)
                    P.ts(Hp, HT_, T2[:, c * L + L - 1:c * L + L], ALU.mult)
                if dbg is not None and dbg == (ti, l) and j == 0:
                    for qi, vv_ in enumerate((SIG, AA, GG, R_, KKt, CUM, BON, KM, AZ[0], BZ[1], KZ[0], rmask)):
                        if qi == 9:
                            continue
                        P.dma("pool", dbgx_d[qi], vv_)
                Y = T1
                P.copy(Y, BANK_Y.all()[:, 0:TT], eng="act")
                b1 = nb()
                P.mm(b1.all()[:, 0:TT], blk, Y)
                P.tt(T3, Y, Y, ALU.mult, eng="pool")
                b2 = nb()
                P.mm(b2.all()[:, 0:TT], blk, T3)
                P.act(T2, b1.all()[:, 0:TT], AF.Copy, scale=1.0 / 64)
                P.act(T3, b2.all()[:, 0:TT], AF.Copy, scale=1.0 / 64)
                P.tt(KM, T2, T2, ALU.mult)
                P.tt(T3, T3, KM, ALU.subtract)
                P.act(T3, T3, AF.Sqrt, bias=64 * 1e-5)
                P.recip(T3, T3)
                P.tt(Y, Y, T2, ALU.subtract)
                P.tt(Y, Y, T3, ALU.mult)
                P.ts(Y, Y, colv(l, O_LNW + j), ALU.mult, colv(l, O_LNB + j), ALU.add)
                P.tt(Y, Y, BON, ALU.add)
                P.tt(yb[1][j], Y, GG, ALU.mult)

            AR.reset()
            AR.views = []
            AB = AR.f32("AB", 16, TT)
            SP = AR.f32("SP", 16, TT)
            SG = pad("SG", 16, TT)
            G16 = pad("G16", 16, TT)
            SEL1 = pad("SEL1", 16, TT)
            SEL2 = pad("SEL2", 16, TT)
            GT = pad("GT", 64, NCH * 16)
            BT = AR.f32("BT", 64, NCH * 16)
            ED = AR.f32("ED", 64, NCH * 16)
            EG = AR.f32("EG", 64, NCH * 16)
            BEG = AR.f32("BEG", 64, NCH * 16)
            P3 = AR.f32("P3", 128, 3 + TT)
            ACC = f("ACC")
            Q_, K_, V_, Z_, KB, QG, GB, EGB, U1, U2 = [f(n) for n in ("Q", "K", "Vg", "Z", "KB", "QG", "GB", "EGB", "U1", "U2")]
            KTMg = AR.f32("KTMg", 64, NCH * 128)
            VTMg = AR.f32("VTMg", 64, NCH * 128)
            KBG = pad("KBG", 64, NCH * 128)
            VB = pad("VB", 64, NCH * 128)
            KD = pad("KD", 64, NCH * 128)
            USg = AR.f32("USg", 64, NCH * 128)
            WT = f("WT")
            e64 = lambda n: AR.f32(n, 64, TT)
            D1 = e64("D1")
            GSU, GU, GSL = e64("GSU"), e64("GU"), e64("GSL")
            N0g, A0g, AQK, TTg = g64("N0g"), g64("A0g"), g64("AQK"), g64("TTg")
            tmpNg = [g64("tNg0"), g64("tNg1")]
            tmpAg = [g64("tAg0"), g64("tAg1")]
            VN = pad("VN", 64, 128)
            fence(prev_views, AR.views)
            prev_views = list(AR.views)
            zero_pads()

            def ev_ab(i, pv, w):
                P.copy(AB, pv, eng="act")
            proj(WIN[l], KT, T_GAB, 1, lambda kt: hT[kt], ev_ab, widths=[16])
            P.act(SP, AB, AF.Exp, bias=V(g16[l].t[:, 0:1], g16[l].trs))
            P.act(SP, SP, AF.Ln, bias=1.0)
            P.ts(SP, SP, V(g16[l].t[:, 1:2], g16[l].trs), ALU.mult)
            P.act(SG, AB, AF.Sigmoid)
            P.scan(G16, rmask[0:16, :], SP)
            b = nb()
            b2 = nb()
            for c in range(NCH):
                cs = slice(c * L, (c + 1) * L)
                P.mm(b.all()[0:64, c * 16:(c + 1) * 16], Fv(G16)[:, cs], ident[:, 0:16])
                P.mm(b2.all()[0:64, c * 16:(c + 1) * 16], Fv(SG)[:, cs], ident[:, 0:16])
            P.copy(GT, b.all()[0:64, 0:NCH * 16], eng="act")
            P.copy(BT, b2.all()[0:64, 0:NCH * 16], eng="act")
            b = nb()
            P.mm(b.all()[0:64, 0:NCH * 16], E63, Fv(GT))
            P.tt(ED, b.all()[0:64, 0:NCH * 16], GT, ALU.subtract)
            P.act(ED, ED, AF.Exp)
            P.act(EG, GT, AF.Exp)
            GT3 = V(GT.ap.rearrange("p (c s) -> p c s", c=NCH), GT.tr)
            BT3 = V(BT.ap.rearrange("p (c s) -> p c s", c=NCH), BT.tr)
            ED3 = V(ED.ap.rearrange("p (c s) -> p c s", c=NCH), ED.tr)
            EG3 = V(EG.ap.rearrange("p (c s) -> p c s", c=NCH), EG.tr)
            BEG3 = V(BEG.ap.rearrange("p (c s) -> p c s", c=NCH), BEG.tr)
            P.tt(BEG3[:, :, 0:8], BT3[:, :, 8:16], EG3[:, :, 0:8], ALU.mult)

            def conv_silu(pv, tix, dst):
                P.copy(P3[:, 3:3 + TT], pv, eng="act")
                P.copy(P3[:, 0:3], V(cvtail[l].t[:, tix, :], cvtail[l].trs), eng="pool")
                P.ts(ACC, P3[:, 0:TT], colv(l, O_CONV + tix), ALU.mult)
                for jj in range(1, 4):
                    P.stt(ACC, P3[:, jj:jj + TT], colv(l, O_CONV + 24 * jj + tix), ACC, ALU.mult, ALU.add)
                P.copy(V(cvtail[l].t[:, tix, :], cvtail[l].trs), P3[:, TT:TT + 3], eng="pool")
                P.act(dst, ACC, AF.Silu)

            def l2n(x_, scale):
                P.tt(U1, x_, x_, ALU.mult, eng="pool")
                bb = nb()
                P.mm(bb.all()[:, 0:TT], ones, U1)
                P.act(U1, bb.all()[:, 0:TT], AF.Sqrt, bias=1e-6)
                P.recip(U1, U1)
                P.stt(x_, x_, scale, U1, ALU.mult, ALU.mult)

            def r3(v, w):
                return V(v.ap.rearrange("p (c s) -> p c s", c=NCH), v.tr)

            for h in range(8):
                proj(WIN[l], KT, T_GQ + h, 1, lambda kt: hT[kt], lambda i, pv, w: conv_silu(pv, h, Q_))
                proj(WIN[l], KT, T_GK + h, 1, lambda kt: hT[kt], lambda i, pv, w: conv_silu(pv, 8 + h, K_))
                proj(WIN[l], KT, T_GV + h, 1, lambda kt: hT[kt], lambda i, pv, w: conv_silu(pv, 16 + h, V_))
                proj(WIN[l], KT, T_GZ + h, 1, lambda kt: hT[kt], lambda i, pv, w: P.act(Z_, pv, AF.Silu))
                l2n(Q_, float(128 ** -0.5))
                l2n(K_, 1.0)
                P.ts(SEL1, G16, ident[0:16, h:h + 1], ALU.mult)
                b = nb()
                P.mm(b.all()[:, 0:TT], ones, Fv(SEL1))
                P.copy(GB, b.all()[:, 0:TT], eng="act")
                P.act(EGB, GB, AF.Exp)
                P.ts(SEL2, SG, ident[0:16, 8 + h:9 + h], ALU.mult)
                b = nb()
                P.mm(b.all()[:, 0:TT], ones, Fv(SEL2))
                P.tt(KB, K_, b.all()[:, 0:TT], ALU.mult)
                P.tt(QG, Q_, EGB, ALU.mult, eng="pool")
                for (dst, s) in ((KTMg, K_), (VTMg, V_)):
                    b = nb()
                    for c in range(NCH):
                        P.mm(b.all()[0:64, c * 128:(c + 1) * 128], s[:, c * L:(c + 1) * L], ident)
                    P.copy(dst, b.all()[0:64, 0:NCH * 128], eng="act")
                K3, V3 = r3(KTMg, 128), r3(VTMg, 128)

                def bc(v3_, col):
                    return V(v3_.ap[:, :, col:col + 1].to_broadcast([64, NCH, 128]), v3_.tr)
                P.tt(r3(KBG, 128), K3, bc(BEG3, h), ALU.mult)
                P.tt(r3(VB, 128), V3, bc(BT3, 8 + h), ALU.mult, eng="pool")
                P.tt(r3(KD, 128), K3, bc(ED3, h), ALU.mult)
                GBr = V(GB.ap[0:64, :].rearrange("p (c s) -> p c s", c=NCH), GB.tr)
                gtb = V(GT3.ap[:, :, h:h + 1].to_broadcast([64, NCH, L]), GT.tr)
                P.tt(r3(D1, L), GBr, gtb, ALU.subtract)
                P.tt(GSU, D1, GB_SU, ALU.add)
                P.act(GSU, GSU, AF.Exp)
                P.tt(GU, D1, GB_U, ALU.add, eng="pool")
                P.act(GU, GU, AF.Exp)
                P.stt(GSL, D1, -1.0, GB_SL, ALU.mult, ALU.add)
                P.act(GSL, GSL, AF.Exp)

                def gmat(dst, lh, rh, gam, neg):
                    b = nb()
                    for c in range(NCH):
                        cs = slice(c * L, (c + 1) * L)
                        P.mm(b.all()[0:64, cs], lh[:, cs], rh[:, cs])
                    if neg:
                        P.stt(dst, b.all()[0:64, 0:TT], -1.0, gam, ALU.mult, ALU.mult)
                    else:
                        P.tt(dst, b.all()[0:64, 0:TT], gam, ALU.mult)
                gmat(N0g, K_, KB, GSU, True)
                gmat(A0g, KB, K_, GSL, True)
                gmat(AQK, K_, Q_, GU, False)
                inv_chain(N0g, A0g, TTg, tmpNg, tmpAg)
                b = nb()
                for c in range(NCH):
                    P.mm(b.all()[0:64, c * 128:(c + 1) * 128], Fv(TTg)[:, c * L:(c + 1) * L], Fv(VB)[:, c * 128:(c + 1) * 128])
                P.copy(USg, b.all()[0:64, 0:NCH * 128], eng="act")
                b = nb()
                for c in range(NCH):
                    P.mm(b.all()[:, c * L:(c + 1) * L], Fv(KBG)[:, c * 128:(c + 1) * 128], Fv(TTg)[:, c * L:(c + 1) * L])
                P.copy(WT, b.all()[:, 0:TT], eng="act")
                S = gdS[l][h]
                for c in range(NCH):
                    cs = slice(c * L, (c + 1) * L)
                    cw = slice(c * 128, (c + 1) * 128)
                    b = nb()
                    P.mm(b.all()[0:64, 0:128], WT[:, cs], S)
                    P.tt(VN, USg[:, cw], b.all()[0:64, 0:128], ALU.subtract)
                    ov = BANK_Y.all()[:, cs]
                    P.mm(ov, S, QG[:, cs], start=True, stop=False)
                    P.mm(ov, Fv(VN), Fv(AQK)[:, cs], start=False, stop=True)
                    P.mm(BANK_H.all()[:, 0:128], Fv(KD)[:, cw], Fv(VN))
                    P.stt(S, S, EGB[:, c * L + L - 1:c * L + L], BANK_H.all()[:, 0:128], ALU.mult, ALU.add)
                O = U2
                P.copy(O, BANK_Y.all()[:, 0:TT], eng="act")
                P.tt(U1, O, O, ALU.mult, eng="pool")
                b = nb()
                P.mm(b.all()[:, 0:TT], ones, U1)
                P.act(U1, b.all()[:, 0:TT], AF.Sqrt, bias=1e-6, scale=1.0 / 128)
                P.recip(U1, U1)
                P.tt(O, O, U1, ALU.mult)
                P.stt(yb[2][h], O, colv(l, O_GNW), Z_, ALU.mult, ALU.mult)

            if dbg is not None and dbg == (ti, l):
                P.dma("pool", dbgh_d, hT.all())
                P.dma("pool", dbgc_d, cols[l].all())
                for br in range(3):
                    P.dma("pool", dbg_d[br], yb[br].all())
            AR.reset()
            AR.views = []
            ACCM = AR.f32("ACCM", 128, KT * TT)
            GS = [AR.f32("GS0", 128, TT), AR.f32("GS1", 128, TT)]
            TM = [AR.f32("TM0", 128, TT), AR.f32("TM1", 128, TT)]
            ST1, ST2 = AR.f32("ST1", 128, TT), AR.f32("ST2", 128, TT)
            ST3 = [AR.f32("ST3a", 128, TT), AR.f32("ST3b", 128, TT)]
            HID = AR.bf16("HID", 128, KTF * TT)
            fence(prev_views, AR.views)
            prev_views = list(AR.views)
            ACC3 = V(ACCM.ap.rearrange("p (k w) -> p k w", k=KT), ACCM.tr)
            HID3 = V(HID.ap.rearrange("p (k w) -> p k w", k=KTF), HID.tr)
            for br in range(3):
                for dt0 in range(0, 16, 3):
                    n = min(3, 16 - dt0)
                    wv_g = w_load(WIN[l], T_GATE + br * 16 + dt0, n, KT)
                    wv_b = w_load(WBR[l][br], dt0, n, 8)
                    for q in range(n):
                        dt = dt0 + q
                        b = nb()
                        for kt in range(KT):
                            P.mm(b.all()[:, 0:TT], wv_g[:, q, kt, :], hT[kt], start=(kt == 0), stop=(kt == KT - 1))
                        g_ = GS[dt % 2]
                        P.act(g_, b.all()[:, 0:TT], AF.Sigmoid)
                        b = nb()
                        for kt in range(8):
                            P.mm(b.all()[:, 0:TT], wv_b[:, q, kt, :], yb[br][kt], start=(kt == 0), stop=(kt == 7))
                        if br == 0:
                            P.tt(ACC3[:, dt, :], g_, b.all()[:, 0:TT], ALU.mult)
                        elif br == 1:
                            t_ = TM[dt % 2]
                            P.tt(t_, g_, b.all()[:, 0:TT], ALU.mult)
                            P.tt(ACC3[:, dt, :], ACC3[:, dt, :], t_, ALU.add, eng="pool")
                        else:
                            t_ = TM[dt % 2]
                            P.tt(t_, g_, b.all()[:, 0:TT], ALU.mult)
                            P.tt(merged[dt], ACC3[:, dt, :], t_, ALU.add, eng="pool")

            def ev_res(o_gate):
                def ev(i, pv, w):
                    t_ = TM[i % 2]
                    P.ts(t_, pv, colv(l, O_MOD + o_gate + i), ALU.mult)
                    P.stt(xT[i], xT[i], DN_ALPHA, t_, ALU.mult, ALU.add, eng="pool")
                return ev
            proj(WOUT[l], KT, 0, 16, lambda kt: merged[kt], ev_res(32))
            layer_norm(l, O_LN1W, O_LN1B, ST1, ST2, ST3)

            for kt in range(KT):
                P.ts(hT[kt], xT[kt], colv(l, O_MOD + 64 + kt), ALU.mult, colv(l, O_MOD + 48 + kt), ALU.add, eng=eng2())

            def ev_up(i, pv, w):
                if i % 2 == 0:
                    P.act(GS[(i // 2) % 2], pv, AF.Silu)
                else:
                    P.tt(HID3[:, i // 2, :], GS[(i // 2) % 2], pv, ALU.mult)
            proj(WUP[l], KT, 0, 2 * KTF, lambda kt: hT[kt], ev_up, per_load=2)
            proj(WDN[l], KTF, 0, 16, lambda kt: HID3[:, kt, :], ev_res(80))
            layer_norm(l, O_LN2W, O_LN2B, ST1, ST2, ST3)

        AR.reset()
        AR.views = []
        OS = AR.f32("OS", 128, 2 * D)
        fence(prev_views, AR.views)
        prev_views = list(AR.views)
        OSv = V(OS.ap.rearrange("p (b d) -> p b d", b=2), OS.tr)
        for tb in range(2):
            for k4 in range(4):
                b = nb()
                for q in range(4):
                    kt = k4 * 4 + q
                    P.mm(b.all()[:, q * 128:(q + 1) * 128], xT[kt][:, tb * 128:(tb + 1) * 128], ident)
                P.copy(OSv[:, tb, k4 * 512:(k4 + 1) * 512], b.all(), eng="act")
        P.dma("pool", out_d[tok0:tok0 + TT, :].rearrange("(b p) d -> p b d", p=128), OSv)

    P.emit()
    P.close()
    return nc, P


_CONSTS = None


def kernel(**inputs):
    global _CONSTS
    x = np.asarray(inputs["x"], np.float32)
    B, T, _ = x.shape
    nc, _ = build(T, 2)
    c128, c64 = make_consts()
    shared = {k: np.ascontiguousarray(np.asarray(v, np.float32)) for k, v in inputs.items() if k not in ("x", "c")}
    shared["rwkv_r_k"] = shared["rwkv_r_k"].reshape(2, C)
    shared.update(c128=c128, c64=c64)
    in_maps = []
    for core in range(8):
        b = core % B
        m = dict(shared)
        m["x"] = np.ascontiguousarray(x[b])
        m["c"] = np.ascontiguousarray(np.asarray(inputs["c"], np.float32)[b])
        in_maps.append(m)
    res = run_bass_kernel_spmd(nc, in_maps, core_ids=list(range(8)))
    return np.stack([res.results[b]["out"] for b in range(B)], axis=0).astype(np.float32)
```
